# Optimizing a Trainium2 kernel written in Bass

```python
import math
import jax, jax.numpy as jnp
from jax import lax
import numpy as np

D_MODEL = 1024
BATCH = 8
SEQ = 2048
DEPTH = 1
DEC_BATCH = 128
DEC_SEQ = 8
PAST_LEN = 16384
PAGE_SIZE = 128

W_LRU = D_MODEL
LRU_BLOCKS = 16
LRU_BLOCK = W_LRU // LRU_BLOCKS
CONV_W = 4
LRU_C = 8.0
GLA_HEADS = 4
GLA_DK = D_MODEL // 2
GLA_DV = D_MODEL
HEAD_K = GLA_DK // GLA_HEADS
HEAD_V = GLA_DV // GLA_HEADS
GK_RANK = 16
GK_TAU = 16.0
GLA_CHUNK = 64
D_MIX = W_LRU + GLA_DV
SPLITS = [W_LRU, W_LRU, GLA_DK, GLA_DK, GLA_DV, GLA_DV, GK_RANK]
D_IN = sum(SPLITS)
EPS = 1e-6

kernel_name = "hymba_rglru_gla_decode_step"


def rmsnorm(x, g):
    xf = x.astype(jnp.float32)
    ms = jnp.mean(xf * xf, axis=-1, keepdims=True)
    return (xf * lax.rsqrt(ms + EPS) * g.astype(jnp.float32)).astype(x.dtype)


def causal_conv(x, buf, w, b):
    L = x.shape[1]
    xp = jnp.concatenate([buf.astype(x.dtype), x], axis=1)
    out = b + sum(xp[:, j:j + L] * w[j] for j in range(CONV_W))
    return out, xp[:, -(CONV_W - 1):]


def rg_lru(x, h0, w_rg, b_rg, w_ig, b_ig, lam):
    Bn, L, _ = x.shape
    xf = x.astype(jnp.float32)
    xb = xf.reshape(Bn, L, LRU_BLOCKS, LRU_BLOCK)
    r = jax.nn.sigmoid(jnp.einsum('blnd,nde->blne', xb, w_rg.astype(jnp.float32)).reshape(Bn, L, W_LRU) + b_rg)
    i = jax.nn.sigmoid(jnp.einsum('blnd,nde->blne', xb, w_ig.astype(jnp.float32)).reshape(Bn, L, W_LRU) + b_ig)
    log_a = LRU_C * r * jax.nn.log_sigmoid(lam.astype(jnp.float32))
    a = jnp.exp(log_a)
    u = jnp.sqrt(-jnp.expm1(2.0 * log_a)) * (i * xf)

    def combine(e1, e2):
        a1, b1 = e1
        a2, b2 = e2
        return a1 * a2, a2 * b1 + b2

    a_cum, h_zero = lax.associative_scan(combine, (a, u), axis=1)
    h = h_zero + a_cum * h0.astype(jnp.float32)[:, None]
    return h.astype(x.dtype), h[:, -1]


def gla(q, k, v, log_alpha, S0):
    Bn, L = q.shape[:2]
    C = math.gcd(L, GLA_CHUNK)
    N = L // C

    def to_chunks(t):
        return t.astype(jnp.float32).reshape(Bn, N, C, GLA_HEADS, t.shape[-1]).transpose(0, 3, 1, 2, 4)

    q, k, v, g = to_chunks(q), to_chunks(k), to_chunks(v), to_chunks(log_alpha)
    b = jnp.cumsum(g, axis=3)
    b_last = b[:, :, :, -1:]
    q_t = q * jnp.exp(b)
    k_t = k * jnp.exp(-b)
    k_end = k * jnp.exp(b_last - b)
    mask = jnp.tril(jnp.ones((C, C), dtype=bool))
    att = jnp.where(mask, jnp.einsum('bhncd,bhnsd->bhncs', q_t, k_t), 0.0)
    o_intra = jnp.einsum('bhncs,bhnsv->bhncv', att, v)
    upd = jnp.einsum('bhncd,bhncv->nbhdv', k_end, v)
    dec = jnp.exp(b_last[:, :, :, 0]).transpose(2, 0, 1, 3)

    def step(S, inp):
        d, u = inp
        return d[..., None] * S + u, S

    S_final, S_starts = lax.scan(step, S0.astype(jnp.float32), (dec, upd))
    o_inter = jnp.einsum('bhncd,nbhdv->bhncv', q_t, S_starts)
    o = (o_intra + o_inter).transpose(0, 2, 3, 1, 4).reshape(Bn, L, GLA_HEADS, HEAD_V)
    return o, S_final


def mixer_layer(x, h0, conv0, S0, g_pre, w_in, conv_w, conv_b, w_rg, b_rg, w_ig, b_ig,
                lru_lambda, w_gk2, b_gk, g_head, w_out, g_post):
    Bn, L, _ = x.shape
    xn = rmsnorm(x, g_pre)
    proj = xn @ w_in
    idx = [int(s) for s in np.cumsum(SPLITS)[:-1]]
    x_lru, z_lru, q, k, v, z_gla, r_gk = jnp.split(proj, idx, axis=-1)
    xc, conv_new = causal_conv(x_lru, conv0, conv_w, conv_b)
    h, h_new = rg_lru(xc, h0, w_rg, b_rg, w_ig, b_ig, lru_lambda)
    y_lru = h * jax.nn.silu(z_lru)
    log_alpha = jax.nn.log_sigmoid((r_gk @ w_gk2 + b_gk).astype(jnp.float32)) / GK_TAU
    q = q.reshape(Bn, L, GLA_HEADS, HEAD_K) * (HEAD_K ** -0.5)
    k = k.reshape(Bn, L, GLA_HEADS, HEAD_K)
    v = v.reshape(Bn, L, GLA_HEADS, HEAD_V)
    log_alpha = log_alpha.reshape(Bn, L, GLA_HEADS, HEAD_K)
    o, S_new = gla(q, k, v, log_alpha, S0)
    o = rmsnorm(o, g_head).reshape(Bn, L, GLA_DV).astype(x.dtype)
    y_gla = o * jax.nn.silu(z_gla)
    y = jnp.concatenate([y_lru, y_gla], axis=-1) @ w_out
    x_out = x + rmsnorm(y, g_post)
    return x_out, h_new.astype(h0.dtype), conv_new.astype(conv0.dtype), S_new.astype(S0.dtype)


def setup_inputs(seed: int = 0) -> dict:
    key = jax.random.key(seed)
    ks = jax.random.split(key, 20)
    f32 = jnp.float32
    nrm = lambda k, s, sc: jax.random.normal(k, s, f32) * sc
    return {
        "x_prompt": nrm(ks[0], (BATCH, SEQ, D_MODEL), 1.0),
        "x_sample": nrm(ks[1], (DEC_BATCH, DEC_SEQ, D_MODEL), 1.0),
        "state_lru_h": nrm(ks[2], (DEPTH, DEC_BATCH, W_LRU), 0.5),
        "state_lru_conv": nrm(ks[3], (DEPTH, DEC_BATCH, CONV_W - 1, W_LRU), 1.0),
        "state_gla": nrm(ks[4], (DEPTH, DEC_BATCH, GLA_HEADS, HEAD_K, HEAD_V), 0.5),
        "g_pre": 1.0 + nrm(ks[5], (DEPTH, D_MODEL), 0.05),
        "w_in": nrm(ks[6], (DEPTH, D_MODEL, D_IN), D_MODEL ** -0.5),
        "conv_w": nrm(ks[7], (DEPTH, CONV_W, W_LRU), CONV_W ** -0.5),
        "conv_b": nrm(ks[8], (DEPTH, W_LRU), 0.01),
        "w_rg": nrm(ks[9], (DEPTH, LRU_BLOCKS, LRU_BLOCK, LRU_BLOCK), LRU_BLOCK ** -0.5),
        "b_rg": nrm(ks[10], (DEPTH, W_LRU), 0.01),
        "w_ig": nrm(ks[11], (DEPTH, LRU_BLOCKS, LRU_BLOCK, LRU_BLOCK), LRU_BLOCK ** -0.5),
        "b_ig": nrm(ks[12], (DEPTH, W_LRU), 0.01),
        "lru_lambda": jax.random.uniform(ks[13], (DEPTH, W_LRU), f32, 4.3, 9.0),
        "w_gk2": nrm(ks[14], (DEPTH, GK_RANK, GLA_DK), GK_RANK ** -0.5),
        "b_gk": jax.random.uniform(ks[15], (DEPTH, GLA_DK), f32, 1.0, 4.0),
        "g_head": 1.0 + nrm(ks[16], (DEPTH, HEAD_V), 0.05),
        "w_out": nrm(ks[17], (DEPTH, D_MIX, D_MODEL), D_MIX ** -0.5),
        "g_post": 1.0 + nrm(ks[18], (DEPTH, D_MODEL), 0.05),
    }


def reference(x_prompt, x_sample, state_lru_h, state_lru_conv, state_gla, g_pre, w_in, conv_w, conv_b,
              w_rg, b_rg, w_ig, b_ig, lru_lambda, w_gk2, b_gk, g_head, w_out, g_post):
    dt = x_prompt.dtype
    xp, xs = x_prompt, x_sample
    hp_l, cp_l, sp_l, hs_l, cs_l, ss_l = [], [], [], [], [], []
    for l in range(DEPTH):
        params = (g_pre[l], w_in[l], conv_w[l], conv_b[l], w_rg[l], b_rg[l], w_ig[l], b_ig[l],
                  lru_lambda[l], w_gk2[l], b_gk[l], g_head[l], w_out[l], g_post[l])
        h0 = jnp.zeros((BATCH, W_LRU), dt)
        c0 = jnp.zeros((BATCH, CONV_W - 1, W_LRU), dt)
        s0 = jnp.zeros((BATCH, GLA_HEADS, HEAD_K, HEAD_V), dt)
        xp, hp, cp, sp = mixer_layer(xp, h0, c0, s0, *params)
        xs, hs, cs, ss = mixer_layer(xs, state_lru_h[l], state_lru_conv[l], state_gla[l], *params)
        hp_l.append(hp); cp_l.append(cp); sp_l.append(sp)
        hs_l.append(hs); cs_l.append(cs); ss_l.append(ss)
    return (xp, xs, jnp.stack(hp_l), jnp.stack(cp_l), jnp.stack(sp_l),
            jnp.stack(hs_l), jnp.stack(cs_l), jnp.stack(ss_l))
```

```python
import numpy as np
from contextlib import ExitStack
import concourse.bass as bass
import concourse.mybir as mybir
from concourse.bass_utils import run_bass_kernel_spmd

F32, BF16 = mybir.dt.float32, mybir.dt.bfloat16
ALU = mybir.AluOpType
AF = mybir.ActivationFunctionType

NCORES = 8
D = 1024
DIN = 5136
LP = 2048
NSEQ = 16
SL = 8
TB = 512
NPB = LP // TB
NTOK = LP + NSEQ * SL
EPS = 1e-6
HK = 128
HV = 256
QSCALE = float(HK) ** -0.5

PC_GPRE, PC_CW, PC_CB, PC_BRG, PC_BIG, PC_LAM, PC_BGK, PC_GH, PC_N = 0, 8, 40, 48, 56, 64, 72, 76, 78
DC_HBRG, DC_HBIG, DC_CH, DC_CF, DC_NBGK, DC_GH2, DC_N = 0, 8, 16, 24, 32, 36, 40
CS_ID, CS_U, CS_US, CS_M, CS_CM, CS_SM, CS_N = 0, 128, 256, 384, 400, 912, 1040


class Res:
    __slots__ = ("name", "w", "r")

    def __init__(self, name):
        self.name = name
        self.w = None
        self.r = []


class _Rec:
    def __init__(self):
        self.calls = []

    def __getattr__(self, name):
        def f(*args, **kwargs):
            self.calls.append((name, args, kwargs))
            return self
        return f


def _ap_elems(ap):
    n = 1
    for d in ap.shape[1:]:
        n *= d
    return n


class Sched:
    ENGS = ("pe", "act", "dve", "pool", "sp")

    def __init__(self, nc, es):
        self.nc = nc
        self.es = es
        self.E = {"pe": nc.tensor, "act": nc.scalar, "dve": nc.vector, "pool": nc.gpsimd, "sp": nc.sync}
        self.nodes = []
        self.pend_pe = None
        self.out_dma_nodes = []
        self.tag = ""

    def _collect(self, reads, writes):
        deps = {}
        for r in reads:
            if r.w is not None:
                deps[r.w] = True
        for w in writes:
            if w.w is not None:
                deps.setdefault(w.w, False)
            for t in w.r:
                deps.setdefault(t, False)
        return deps

    def _cost(self, eng, name, kwargs):
        try:
            if eng == "pe":
                if name == "transpose":
                    return 0.11
                n = _ap_elems(kwargs["rhs"])
                return 0.06 + n * 0.00039
            out = kwargs.get("out", None)
            if out is None:
                out = kwargs.get("ap", None)
            n = _ap_elems(out) if out is not None else 512
            if eng == "act":
                return 0.22 + n * 0.00085
            if eng == "dve":
                if name == "tensor_tensor_scan":
                    return 0.25 + n * 0.0021
                return 0.15 + n * 0.00105
            if eng == "pool":
                return 0.2 + n * 0.0032
        except Exception:
            pass
        return 0.5

    def op(self, eng, fn, reads=(), writes=(), signal=True):
        rec = _Rec()
        fn(rec)
        assert len(rec.calls) == 1
        name, args, kwargs = rec.calls[0]
        deps = self._collect(reads, writes)
        cost = self._cost(eng, name, kwargs)
        if eng == "pe" and self.pend_pe is not None:
            node = self.pend_pe
            node["calls"].append((name, args, kwargs))
            for d, raw in deps.items():
                if d != node["id"]:
                    node["deps"][d] = node["deps"].get(d, False) or raw
            node["cost"] += cost
        else:
            node = {"id": len(self.nodes), "eng": eng, "calls": [(name, args, kwargs)], "deps": dict(deps),
                    "cost": cost, "dma": False, "tset": None, "tag": self.tag}
            if eng == "act" and name == "activation":
                f = kwargs.get("func")
                node["tset"] = {AF.Tanh: "T", AF.Ln: "L", AF.Sqrt: "S"}.get(f, None)
            self.nodes.append(node)
        nid = node["id"]
        if not signal:
            assert eng == "pe"
            self.pend_pe = node
        else:
            self.pend_pe = None
        for r in reads:
            if nid not in r.r:
                r.r.append(nid)
        for w in writes:
            w.w = nid
            w.r = []
        return nid

    def dma(self, q, out, in_, reads, writes, semkey, is_output=False):
        assert self.pend_pe is None or q != "pe"
        deps = self._collect(reads, writes)
        nbytes = 128 * _ap_elems(out) * (2 if out.dtype == BF16 else 4)
        node = {"id": len(self.nodes), "eng": q, "calls": [("dma_start", (), {"out": out, "in_": in_})],
                "deps": dict(deps), "cost": 0.12 if q == "sp" else 1.0, "dma": True, "semkey": semkey, "bytes": nbytes,
                "tag": self.tag}
        self.nodes.append(node)
        nid = node["id"]
        if is_output:
            self.out_dma_nodes.append(nid)
        for r in reads:
            if nid not in r.r:
                r.r.append(nid)
        for w in writes:
            w.w = nid
            w.r = []
        return nid

    def _schedule(self):
        nodes = self.nodes
        n = len(nodes)
        nsucc = [[] for _ in range(n)]
        indeg = [0] * n
        for nd in nodes:
            for d in nd["deps"]:
                nsucc[d].append(nd["id"])
                indeg[nd["id"]] += 1
        last_sem = {}
        for nd in nodes:
            if nd["dma"]:
                k = nd["semkey"]
                if k in last_sem and last_sem[k] not in nd["deps"]:
                    nd["deps"][last_sem[k]] = False
                    nsucc[last_sem[k]].append(nd["id"])
                    indeg[nd["id"]] += 1
                last_sem[k] = nd["id"]
        bl = [0.0] * n
        for nd in reversed(nodes):
            i = nd["id"]
            c = (2.0 + nd["bytes"] / 240e3) if nd["dma"] else nd["cost"] + 0.15
            m = 0.0
            for j in nsucc[i]:
                if bl[j] > m:
                    m = bl[j]
            bl[i] = c + m
        self.bl = bl
        PRI_WIN = getattr(Sched, "PRI_WIN", 0.3)
        finish = [0.0] * n
        ready_t = [0.0] * n
        free_at = {e: 0.0 for e in self.ENGS}
        pipe_t = 0.0
        import heapq
        ready = {e: [] for e in self.ENGS}
        for nd in nodes:
            if indeg[nd["id"]] == 0:
                heapq.heappush(ready[nd["eng"]], (0.0, nd["id"]))
        order = []
        done = 0
        cur_set = [None]
        SWITCH = getattr(Sched, "SWITCH", 1.4)
        ACT_WIN = getattr(Sched, "ACT_WIN", 2.5)
        while done < n:
            best = None
            for e in self.ENGS:
                h = ready[e]
                if not h:
                    continue
                base = max(free_at[e], h[0][0])
                lim = base + (ACT_WIN if e == "act" else PRI_WIN + 1e-9)
                cand = None
                for (rt, i) in h:
                    if rt > lim:
                        continue
                    pen = 0.0
                    if e == "act":
                        ts_ = nodes[i]["tset"]
                        if ts_ is not None and cur_set[0] is not None and ts_ != cur_set[0]:
                            pen = SWITCH
                    stt = max(free_at[e], rt) + pen
                    if PRI_WIN > 0:
                        key = (round(stt / PRI_WIN) if e != "act" else stt, -bl[i], i)
                    else:
                        key = (stt, i)
                    if cand is None or key < cand[0]:
                        cand = (key, rt, i, pen, stt)
                st = cand[4] if PRI_WIN > 0 else cand[0][0]
                if best is None or st < best[0] or (st == best[0] and cand[2] < best[2]):
                    best = (st, e, cand[2], cand[1])
            st, e, i, rt = best
            ready[e].remove((rt, i))
            heapq.heapify(ready[e])
            nd = nodes[i]
            if e == "act" and nd["tset"] is not None:
                cur_set[0] = nd["tset"]
            if nd["dma"]:
                free_at[e] = st + nd["cost"]
                pipe_t = max(pipe_t, st) + nd["bytes"] / getattr(Sched, "DMA_BPUS", 240e3)
                finish[i] = max(st + 2.0, pipe_t + 1.5)
            else:
                free_at[e] = st + nd["cost"]
                finish[i] = free_at[e] + 0.15
            nd["start"] = st
            nd["fin"] = finish[i]
            order.append(i)
            done += 1
            for j in nsucc[i]:
                indeg[j] -= 1
                if finish[i] > ready_t[j]:
                    ready_t[j] = finish[i]
                if indeg[j] == 0:
                    heapq.heappush(ready[nodes[j]["eng"]], (ready_t[j], j))
        self.est_total = max(finish) if finish else 0.0
        return order

    def finish(self):
        assert self.pend_pe is None
        order = self._schedule()
        nc, es = self.nc, self.es
        sems, cnt, hist, dcnt = {}, {}, {}, {}
        seen = {k: {} for k in self.ENGS}
        for k in ("pe", "act", "dve", "pool"):
            sems[k] = es.enter_context(nc.semaphore("s_" + k))
            cnt[k] = 0
            hist[k] = [{}]
        tok = {}
        self.nwait = 0

        def wait(eng, t):
            key, val = t
            if seen[eng].get(key, 0) >= val:
                return
            self.E[eng].wait_ge(sems[key], val)
            self.nwait += 1
            seen[eng][key] = val
            if key in hist and val < len(hist[key]):
                for k2, v2 in hist[key][val].items():
                    if seen[eng].get(k2, 0) < v2:
                        seen[eng][k2] = v2

        EMBED = getattr(Sched, "EMBED", True)

        def mark_seen(eng, t):
            key, val = t
            seen[eng][key] = val
            if key in hist and val < len(hist[key]):
                for k2, v2 in hist[key][val].items():
                    if seen[eng].get(k2, 0) < v2:
                        seen[eng][k2] = v2

        for i in order:
            nd = self.nodes[i]
            eng = nd["eng"]
            need = {}
            for d, raw in nd["deps"].items():
                t = tok[d]
                if need.get(t[0], 0) < t[1]:
                    need[t[0]] = t[1]
            items = sorted(need.items(), key=lambda kv: (kv[0] == eng, kv[0] not in hist, kv[0]))
            snap = dict(seen[eng])
            pend = []
            for k, v in items:
                if seen[eng].get(k, 0) >= v:
                    continue
                pend.append((k, v))
                mark_seen(eng, (k, v))
            seen[eng] = snap
            emb = None
            multi = any(("accum_out" in c[2] and c[2]["accum_out"] is not None) for c in nd["calls"][:1])
            if EMBED and pend and not nd["dma"] and not multi:
                emb = pend.pop()
            for k, v in pend:
                wait(eng, (k, v))
            inst = None
            for ci, (name, args, kwargs) in enumerate(nd["calls"]):
                inst = getattr(self.E[eng], name)(*args, **kwargs)
                if ci == 0 and emb is not None:
                    inst._wait_ge(sems[emb[0]], emb[1])
                    mark_seen(eng, emb)
            if nd["dma"]:
                key = nd["semkey"]
                if key not in sems:
                    sems[key] = es.enter_context(nc.semaphore("d_" + key))
                    dcnt[key] = 0
                inst.then_inc(sems[key], 16)
                dcnt[key] += 16
                tok[i] = (key, dcnt[key])
            else:
                inst.then_inc(sems[eng], 1)
                cnt[eng] += 1
                hist[eng].append(dict(seen[eng]))
                tok[i] = (eng, cnt[eng])
        for i in self.out_dma_nodes:
            wait("sp", tok[i])
        for k in ("pe", "act", "dve", "pool"):
            wait("sp", (k, cnt[k]))


def build_nc():
    nc = bass.Bass("TRN2", target_bir_lowering=False)
    dt_in = lambda name, shape: nc.dram_tensor(name, list(shape), F32, kind="ExternalInput").ap()
    dt_out = lambda name, shape: nc.dram_tensor(name, list(shape), F32, kind="ExternalOutput").ap()
    x_d = dt_in("x", [NTOK, D])
    win_d = dt_in("w_in", [D, DIN])
    wout_d = dt_in("w_out", [2 * D, D])
    wg_d = dt_in("wg", [128, 2 * 8 * 128])
    wgk2_d = dt_in("wgk2", [16, 512])
    pcols_d = dt_in("pcols", [128, PC_N])
    cst_d = dt_in("cst", [128, CS_N])
    gpost_d = dt_in("gpost", [128, D])
    shc_d = dt_in("shc", [128, 8 * NSEQ * 3])
    shh_d = dt_in("shh", [128, 8 * NSEQ])
    sgla_d = dt_in("sgla", [NSEQ, 4, 128, 256])

    y_d = dt_out("y", [NTOK, D])
    oh_d = dt_out("o_h", [128, 8 * 17])
    oc_d = dt_out("o_c", [128, 8 * 17 * 3])
    os_d = dt_out("o_s", [17, 4, 128, 256])
    wbf_d = nc.dram_tensor("wbf_scratch", [128, 41, 8, 128], BF16).ap()
    win_v = win_d.rearrange("(kc p) n -> p kc n", p=128)
    wout_v = wout_d.rearrange("(kc p) n -> p kc n", p=128)

    with ExitStack() as es:
        S = Sched(nc, es)
        op, dma = S.op, S.dma

        def sb(name, shape, dtype):
            return es.enter_context(nc.sbuf_tensor("sb_" + name, list(shape), dtype))

        def ps(name, shape, dtype):
            return es.enter_context(nc.psum_tensor("ps_" + name, list(shape), dtype))

        xT = [sb(f"xT{i}", [128, 8, TB], BF16) for i in range(2)]
        r_xT = [Res(f"xT{i}") for i in range(2)]
        yTL = [sb(f"yTL{i}", [128, 8, TB], BF16) for i in range(2)]
        r_yTL = [[Res(f"yTL{i}_{k}") for k in range(8)] for i in range(2)]
        yTG = sb("yTG", [128, 8, TB], BF16)
        r_yTG = [Res(f"yTG{k}") for k in range(8)]
        wout_bf = sb("wout_bf", [128, 16, D], BF16)
        r_wout_k = [Res(f"wout{g}") for g in range(16)]
        NWL = getattr(Sched, "NWL", 4)
        wl = [sb(f"wl{i}", [128, 8, 256], BF16) for i in range(NWL)]
        r_wl = [Res(f"wl{i}") for i in range(NWL)]
        NXI = 2
        xin = [sb(f"xin{i}", [128, D], F32) for i in range(NXI)]
        r_xin = [Res(f"xin{i}") for i in range(NXI)]
        osb = [sb(f"osb{i}", [128, D], F32) for i in range(1)]
        r_osb = [Res(f"osb{i}") for i in range(1)]
        xsb = [sb(f"xsb{i}", [128, D], BF16) for i in range(2)]
        r_xsb = [Res(f"xsb{i}") for i in range(2)]
        junk = sb("junk", [128, 512], BF16)
        r_junk = Res("junk")
        small = sb("small", [128, 64], F32)
        pcols = sb("pcols", [128, PC_N], F32)
        dcols = sb("dcols", [128, DC_N], F32)
        cst = sb("cst", [128, CS_N], F32)
        gpost = sb("gpost", [128, D], F32)
        wg_bf = sb("wg_bf", [128, 2 * 8 * 128], BF16)
        wgk2_bf = sb("wgk2_bf", [16, 512], BF16)
        ident_bf = sb("ident_bf", [128, 128], BF16)
        ones_bf = sb("ones_bf", [128, 128], BF16)
        shc = sb("shc", [128, 8, NSEQ, 3], F32)
        shh = sb("shh", [128, 8, NSEQ], F32)
        hst = sb("hst", [128, 8], F32)
        cst3 = sb("cst3", [128, 8, 3], F32)
        hfin = sb("hfin", [128, 8, 17], F32)
        cfin = sb("cfin", [128, 8, 17, 3], F32)
        Sst = sb("Sst", [128, 4, HV], F32)
        Sbf = sb("Sbf", [128, 4, HV], BF16)
        r_const = Res("const")
        r_dcols = Res("dcols")
        r_wgbf = Res("wgbf")
        r_wgk2 = Res("wgk2bf")
        r_idb = Res("identbf")
        r_hst = [Res(f"hst{c}") for c in range(8)]
        r_cst3 = [Res(f"cst3{c}") for c in range(8)]
        r_fin = Res("fin")
        r_S = [Res(f"S{h}") for h in range(4)]
        r_Sbf = [Res(f"Sbf{h}") for h in range(4)]

        def tmpf(name, n=TB, k=2, dtype=F32):
            return [sb(f"{name}{i}", [128, n], dtype) for i in range(k)], [Res(f"{name}{i}") for i in range(k)]

        XL, r_XL = tmpf("XL", TB + 4, 2, F32)
        TR, r_TR = tmpf("TR")
        TI, r_TI = tmpf("TI")
        A2, r_A2 = tmpf("A2")
        AA, r_AA = tmpf("AA")
        UU, r_UU = tmpf("UU")
        HH, r_HH = tmpf("HH")
        TZ, r_TZ = tmpf("TZ", TB, 2, BF16)
        XCB, r_XCB = tmpf("XCB", TB, 2, BF16)
        SZ, r_SZ = tmpf("SZ", TB, 2, BF16)
        EEb = sb("EEb", [128, TB], F32)
        BBb = sb("BBb", [128, TB], F32)
        E123 = [sb(f"E{i}b", [128, TB], BF16) for i in range(3)]
        r_EEb, r_BBb, r_E123 = Res("EEb"), Res("BBb"), [Res(f"E{i}b") for i in range(3)]
        XC, r_XC = tmpf("XC")
        XLs = sb("XLs", [128, NSEQ, 11], F32)
        AAs = sb("AAs", [128, NSEQ, 9], F32)
        UUs = sb("UUs", [128, NSEQ, 9], F32)
        HHs = sb("HHs", [128, NSEQ, 9], F32)
        r_XLs, r_AAs, r_UUs, r_HHs = Res("XLs"), Res("AAs"), Res("UUs"), Res("HHs")
        RGK = sb("RGK", [16, TB], BF16)
        r_RGK = Res("RGK")
        QT = sb("QT", [128, TB], BF16)
        KT = sb("KT", [128, TB], BF16)
        KE = sb("KE", [128, TB], BF16)
        r_QT, r_KT, r_KE = Res("QT"), Res("KT"), Res("KE")
        VS = sb("VS", [128, 4, HV], BF16)
        r_VS = [Res(f"VS{j}") for j in range(4)]
        DEC = sb("DEC", [128, 16], F32)
        r_DEC = Res("DEC")
        KEt = [sb(f"KEt{i}", [128, 128], BF16) for i in range(2)]
        r_KEt = [Res(f"KEt{i}") for i in range(2)]
        ATT = [sb(f"ATT{i}", [128, 128], BF16) for i in range(2)]
        r_ATT = [Res(f"ATT{i}") for i in range(2)]
        SQ = [sb(f"SQ{i}", [128, 256], BF16) for i in range(2)]
        r_SQ = [Res(f"SQ{i}") for i in range(2)]
        RT = [sb(f"RT{i}", [128, 128], F32) for i in range(2)]
        r_RT = [Res(f"RT{i}") for i in range(2)]
        RS = [sb(f"RS{i}", [128, 128], F32) for i in range(2)]
        r_RS = [Res(f"RS{i}") for i in range(2)]
        Y1 = [sb(f"Y1{i}", [128, 128], F32) for i in range(2)]
        r_Y1 = [Res(f"Y1{i}") for i in range(2)]
        arena = sb("arena", [128, 4096], F32)
        r_ax = [Res(f"ax{i}") for i in range(4)]
        QM = arena[:, 0:1024].bitcast(BF16)
        KEM = arena[:, 1024:2048].bitcast(BF16)
        r_QM, r_KEM = Res("QM"), Res("KEM")
        NS0 = 8
        S0 = [arena[:, 2048 + i * 256:2048 + (i + 1) * 256] for i in range(NS0)]
        r_S0 = [Res(f"S0{i}") for i in range(NS0)]
        NSB = 4
        S0b = [sb(f"S0b{i}", [128, HV], BF16) for i in range(NSB)]
        r_S0b = [Res(f"S0b{i}") for i in range(NSB)]

        bank = [ps(f"pb{i}", [128, 512], F32) for i in range(7)]
        r_bank = [Res(f"pb{i}") for i in range(7)]
        ptr = ps("ptr", [128, 1024], BF16)
        r_ptr = Res("ptr")
        bank.append(ptr[:].bitcast(F32))
        r_bank.append(r_ptr)
        ring_ids = [list(range(7))]
        ring_i = [0]

        def ring_next():
            ids = ring_ids[0]
            i = ids[ring_i[0] % len(ids)]
            ring_i[0] += 1
            return bank[i], r_bank[i]

        pb_v, pb_a, pb_o0, pb_o1 = bank[3], bank[4], bank[5], bank[6]
        r_pbv = [r_bank[3], r_bank[3]]
        r_patt = r_bank[4]
        r_pss = r_bank[4]
        r_po = [r_bank[5], r_bank[6]]

        pc = lambda c: pcols[:, c:c + 1]
        dc = lambda c: dcols[:, c:c + 1]
        sm = lambda c, n=1: small[:, c:c + n]
        r_sm = {}

        def rsm(c):
            if c not in r_sm:
                r_sm[c] = Res(f"small{c}")
            return r_sm[c]

        r_cparts = []
        for (dst, src) in ((pcols[:], pcols_d), (cst[:], cst_d), (gpost[:], gpost_d),                            (shc[:].rearrange("p a b c -> p (a b c)"), shc_d), (shh[:].rearrange("p a b -> p (a b)"), shh_d)):
            rc_ = Res("c_" + str(len(r_cparts)))
            r_cparts.append(rc_)
            dma("sp", dst, src, [], [rc_], f"const{len(r_cparts)}")
        op("pool", lambda e: e.memset(small[:, 63:64], 0.0), r_cparts, [r_const])
        dma("pool", ident_bf[:], cst_d[:, CS_ID:CS_ID + 128], [], [r_idb], "identq")

        def late_setup():
          dma("pool", wg_bf[:], wg_d, [], [r_wgbf], "wgbf")
          dma("pool", wgk2_bf[:], wgk2_d, [], [r_wgk2], "wgk2")
          op("dve", lambda e: e.memset(ones_bf[:], 1.0), [], [r_idb])
          op("dve", lambda e: e.memset(hst[:], 0.0), [], r_hst)
          op("dve", lambda e: e.memset(cst3[:].rearrange("p a b -> p (a b)"), 0.0), [], r_cst3)
          op("dve", lambda e: e.memset(Sst[:].rearrange("p a b -> p (a b)"), 0.0), [], r_S)
          op("dve", lambda e: e.memset(Sbf[:].rearrange("p a b -> p (a b)"), 0.0), [], r_Sbf)
          op("dve", lambda e: e.memset(AAs[:].rearrange("p a b -> p (a b)"), 0.0), [], [r_AAs])
          op("dve", lambda e: e.memset(UUs[:].rearrange("p a b -> p (a b)"), 0.0), [], [r_UUs])
          op("dve", lambda e: e.tensor_scalar(out=dcols[:, DC_HBRG:DC_HBRG + 16], in0=pcols[:, PC_BRG:PC_BRG + 16],
                                              scalar1=0.5, scalar2=None, op0=ALU.mult), [r_const], [r_dcols])
          op("dve", lambda e: e.tensor_scalar(out=dcols[:, DC_NBGK:DC_NBGK + 4], in0=pcols[:, PC_BGK:PC_BGK + 4],
                                              scalar1=-1.0, scalar2=None, op0=ALU.mult), [r_const], [r_dcols])
          op("dve", lambda e: e.tensor_scalar(out=dcols[:, DC_GH2:DC_GH2 + 2], in0=pcols[:, PC_GH:PC_GH + 2],
                                              scalar1=0.5, scalar2=None, op0=ALU.mult), [r_const], [r_dcols])
          op("act", lambda e: e.activation(out=sm(0, 8), in_=pcols[:, PC_LAM:PC_LAM + 8], func=AF.Exp, scale=-1.0),
             [r_const], [rsm(0)])
          op("act", lambda e: e.activation(out=sm(8, 8), in_=sm(0, 8), func=AF.Ln, bias=1.0), [rsm(0)], [rsm(8)])
          op("dve", lambda e: e.tensor_scalar(out=dcols[:, DC_CH:DC_CH + 8], in0=sm(8, 8), scalar1=-4.0, scalar2=None,
                                              op0=ALU.mult), [rsm(8)], [r_dcols])
          op("dve", lambda e: e.tensor_scalar(out=dcols[:, DC_CF:DC_CF + 8], in0=sm(8, 8), scalar1=-8.0, scalar2=None,
                                              op0=ALU.mult), [rsm(8)], [r_dcols])

        r_wd = {}
        gpre_bc = pcols[:, PC_GPRE:PC_GPRE + 8].unsqueeze(2).to_broadcast([128, 8, 128])
        wl_i = [0]
        last_p = [False]
        first_pass = [True]

        def load_w(parts):
            s = wl_i[0] % NWL
            wl_i[0] += 1
            off = 0
            for (c0, n) in parts:
                for tt in range((n + 127) // 128):
                    t = c0 // 128 + tt
                    nn = min(128, n - tt * 128)
                    if first_pass[0]:
                        cc = c0 + tt * 128
                        dma("pool", wl[s][:, :, off:off + nn], win_v[:, :, cc:cc + nn], [], [r_wl[s]], f"wlq{s}")
                        r_wd[t] = Res(f"wd{t}")
                        dma("sp", wbf_d[:, t, :, 0:nn], wl[s][:, :, off:off + nn], [r_wl[s]], [r_wd[t]], f"wds{s}")
                    else:
                        dma("sp", wl[s][:, :, off:off + nn], wbf_d[:, t, :, 0:nn], [r_wd[t]], [r_wl[s]], f"wl{s}")
                    off += nn
            return wl[s], r_wl[s]

        wo_i = [0]

        def load_wout(k):
            for _ in range(k):
                g = wo_i[0]
                if g >= 16:
                    return
                wo_i[0] += 1
                dma("pool", wout_bf[:, g, :], wout_v[:, g, :], [], [r_wout_k[g]], f"wout{g % 4}")

        def proj(w, r_w, woff, m, xTt, r_x, c0, n, out, r_out):
            for kc in range(8):
                op("pe", lambda e, kc=kc: e.matmul(out, lhsT=w[:, kc, woff:woff + m], rhs=xTt[:, kc, c0:c0 + n],
                                                    start=(kc == 0), stop=(kc == 7)),
                   [r_w, r_x], [r_out], signal=(kc == 7))

        xi_i = [0]

        def phase0(bi, row0, ntile):
            xTt, r_x = xT[bi % 2], r_xT[bi % 2]
            for j in range(ntile):
                s = xi_i[0] % NXI
                xi_i[0] += 1
                s2 = s % 2
                rows = slice(row0 + j * 128, row0 + (j + 1) * 128)
                if bi == 0:
                    xsrc, r_xs = arena[:, j * 1024:(j + 1) * 1024], r_ax[j]
                    dma("sp", xsrc, x_d[rows, :], [], [r_xs], f"ax{j}")
                else:
                    xsrc, r_xs = xin[s][:], r_xin[s]
                    dma("sp", xsrc, x_d[rows, :], [], [r_xs], f"xin{s}")
                op("dve", lambda e, xsrc=xsrc, s2=s2: e.scalar_tensor_tensor(out=xsb[s2][:], in0=xsrc, scalar=1.0, in1=xsrc,
                                                                        op0=ALU.mult, op1=ALU.mult, accum_out=sm(16)),
                   [r_xs], [r_xsb[s2], rsm(16)])
                op("act", lambda e: e.activation(out=sm(17), in_=sm(16), func=AF.Ln, scale=1.0 / D, bias=EPS),
                   [rsm(16)], [rsm(17)])
                op("act", lambda e: e.activation(out=sm(18), in_=sm(17), func=AF.Exp, scale=-0.5), [rsm(17)], [rsm(18)])
                op("dve", lambda e, xsrc=xsrc, s2=s2: e.tensor_scalar(out=xsb[s2][:], in0=xsrc, scalar1=sm(18),
                                                                 scalar2=None, op0=ALU.mult),
                   [r_xs, rsm(18)], [r_xsb[s2]])
                for kc in range(8):
                    op("pe", lambda e, kc=kc, s2=s2: e.transpose(out=ptr[:, kc * 128:(kc + 1) * 128],
                                                                  in_=xsb[s2][:, kc * 128:(kc + 1) * 128],
                                                                  identity=ident_bf[:]),
                       [r_xsb[s2], r_idb], [r_ptr], signal=(kc == 7))
                op("dve", lambda e, j=j: e.tensor_tensor(out=xTt[:, :, j * 128:(j + 1) * 128],
                                                        in0=ptr[:].rearrange("p (a b) -> p a b", a=8), in1=gpre_bc,
                                                        op=ALU.mult), [r_ptr, r_const], [r_x])

        def phase1(bi, kind):
            xTt, r_x = xT[bi % 2], r_xT[bi % 2]
            n = TB if kind == "p" else 128
            ring_ids[0] = list(getattr(Sched, "RING1", [0, 1, 2, 3, 4]))
            v3 = lambda ap: ap.rearrange("p (a b) -> p a b", b=8)
            for ct in range(8):
                k = ct % 2
                w, r_w = load_w([(ct * 128, 128), (1024 + ct * 128, 128)])
                px, r_px = ring_next()
                proj(w, r_w, 0, 128, xTt, r_x, 0, n, px[:, 0:n], r_px)
                pz, r_pz = ring_next()
                proj(w, r_w, 128, 128, xTt, r_x, 0, n, pz[:, 0:n], r_pz)
                if kind == "p":
                    xl, r_xl = XL[k], r_XL[k]
                    op("pool", lambda e: e.tensor_copy(out=xl[:, 0:3], in_=cst3[:, ct, :]), [r_cst3[ct]], [r_xl])
                    op("act", lambda e: e.activation(out=xl[:, 3:3 + n], in_=px[:, 0:n], func=AF.Copy), [r_px], [r_xl])
                    op("pool", lambda e: e.tensor_copy(out=cst3[:, ct, :], in_=xl[:, n:n + 3]), [r_xl], [r_cst3[ct]])
                    if last_p[0]:
                        op("act", lambda e: e.activation(out=cfin[:, ct, 0, :], in_=px[:, n - 3:n], func=AF.Copy),
                           [r_px], [r_fin])
                    taps = [xl[:, j:j + n] for j in range(4)]
                else:
                    xl, r_xl = XLs, r_XLs
                    op("pool", lambda e: e.tensor_copy(out=XLs[:, :, 0:3], in_=shc[:, ct, :, :]), [r_const], [r_xl])
                    op("act", lambda e: e.activation(out=XLs[:, :, 3:11], in_=v3(px[:, 0:n]), func=AF.Copy), [r_px], [r_xl])
                    op("act", lambda e: e.activation(out=cfin[:, ct, 1:17, :], in_=v3(px[:, 0:n])[:, :, 5:8], func=AF.Copy),
                       [r_px], [r_fin])
                    taps = [XLs[:, :, j:j + 8] for j in range(4)]
                xc_v = XC[k][:, 0:n] if kind == "p" else v3(XC[k][:, 0:n])
                pcv, r_pc = XC[k], r_XC[k]
                op("act", lambda e: e.activation(out=XC[k][:, 0:n], in_=px[:, 0:n], func=AF.Identity,
                                                 scale=pc(PC_CW + 3 * 8 + ct), bias=pc(PC_CB + ct)), [r_px, r_const], [r_XC[k]])
                for j in range(3):
                    wj = pc(PC_CW + j * 8 + ct)
                    op("dve", lambda e, j=j, wj=wj: e.scalar_tensor_tensor(out=xc_v, in0=taps[j], scalar=wj, in1=xc_v,
                                                                       op0=ALU.mult, op1=ALU.add),
                       [r_xl, r_XC[k], r_const], [r_XC[k]])
                op("dve", lambda e: e.tensor_copy(out=XCB[k][:, 0:n], in_=XC[k][:, 0:n]), [r_XC[k]], [r_XCB[k]])
                op("act", lambda e: e.activation(out=TZ[k][:, 0:n], in_=pz[:, 0:n], func=AF.Tanh, scale=0.5),
                   [r_pz], [r_TZ[k]])
                op("dve", lambda e: e.scalar_tensor_tensor(out=SZ[k][:, 0:n], in0=TZ[k][:, 0:n], scalar=1.0,
                                                           in1=pz[:, 0:n], op0=ALU.add, op1=ALU.mult),
                   [r_TZ[k], r_pz], [r_SZ[k]])
                pr_, r_pr = ring_next()
                op("pe", lambda e: e.matmul(pr_[:, 0:n], lhsT=wg_bf[:, ct * 128:(ct + 1) * 128], rhs=XCB[k][:, 0:n],
                                            start=True, stop=True), [r_wgbf, r_XCB[k]], [r_pr])
                pi_, r_pi = ring_next()
                op("pe", lambda e: e.matmul(pi_[:, 0:n], lhsT=wg_bf[:, (8 + ct) * 128:(9 + ct) * 128], rhs=XCB[k][:, 0:n],
                                            start=True, stop=True), [r_wgbf, r_XCB[k]], [r_pi])
                op("act", lambda e: e.activation(out=TR[k][:, 0:n], in_=pr_[:, 0:n], func=AF.Tanh, scale=0.5,
                                                 bias=dc(DC_HBRG + ct)), [r_pr, r_dcols], [r_TR[k]])
                op("act", lambda e: e.activation(out=TI[k][:, 0:n], in_=pi_[:, 0:n], func=AF.Tanh, scale=0.5,
                                                 bias=dc(DC_HBIG + ct)), [r_pi, r_dcols], [r_TI[k]])
                op("dve", lambda e: e.scalar_tensor_tensor(out=TI[k][:, 0:n], in0=TI[k][:, 0:n], scalar=1.0,
                                                           in1=pcv[:, 0:n], op0=ALU.add, op1=ALU.mult),
                   [r_TI[k], r_pc], [r_TI[k]])
                if kind == "p":
                    a_out, r_a = AA[k][:, 0:n], r_AA[k]
                    u_out, r_u = UU[k][:, 0:n], r_UU[k]
                    a_in, m_in, ip_in = TR[k][:, 0:n], A2[k][:, 0:n], TI[k][:, 0:n]
                else:
                    a_out, r_a = AAs[:, :, 1:9], r_AAs
                    u_out, r_u = UUs[:, :, 1:9], r_UUs
                    a_in, m_in, ip_in = v3(TR[k][:, 0:n]), v3(A2[k][:, 0:n]), v3(TI[k][:, 0:n])
                op("act", lambda e: e.activation(out=a_out, in_=a_in, func=AF.Exp, scale=dc(DC_CH + ct),
                                                 bias=dc(DC_CH + ct)), [r_TR[k], r_dcols], [r_a])
                op("dve", lambda e: e.tensor_tensor(out=m_in, in0=a_out, in1=a_out, op=ALU.mult), [r_a], [r_A2[k]])
                op("act", lambda e: e.activation(out=A2[k][:, 0:n], in_=A2[k][:, 0:n], func=AF.Ln, scale=-0.25,
                                                 bias=0.25), [r_A2[k]], [r_A2[k]])
                op("act", lambda e: e.activation(out=A2[k][:, 0:n], in_=A2[k][:, 0:n], func=AF.Exp, scale=0.5),
                   [r_A2[k]], [r_A2[k]])
                op("pool", lambda e: e.tensor_tensor(out=u_out, in0=m_in, in1=ip_in, op=ALU.mult),
                   [r_A2[k], r_TI[k]], [r_u])
                if kind == "p":
                    op("dve", lambda e: e.tensor_tensor_scan(out=HH[k][:, 0:n], data0=AA[k][:, 0:n], data1=UU[k][:, 0:n],
                                                             initial=hst[:, ct:ct + 1], op0=ALU.mult, op1=ALU.add),
                       [r_AA[k], r_UU[k], r_hst[ct]], [r_HH[k]])
                    op("pool", lambda e: e.tensor_copy(out=hst[:, ct:ct + 1], in_=HH[k][:, n - 1:n]),
                       [r_HH[k]], [r_hst[ct]])
                    op("dve", lambda e: e.scalar_tensor_tensor(out=yTL[bi % 2][:, ct, 0:n], in0=HH[k][:, 0:n], scalar=0.5,
                                                               in1=SZ[k][:, 0:n], op0=ALU.mult, op1=ALU.mult),
                       [r_HH[k], r_SZ[k]], [r_yTL[bi % 2][ct]])
                    if last_p[0]:
                        op("pool", lambda e: e.tensor_copy(out=hfin[:, ct, 0:1], in_=HH[k][:, n - 1:n]),
                           [r_HH[k]], [r_fin])
                else:
                    op("pool", lambda e: e.tensor_copy(out=UUs[:, :, 0], in_=shh[:, ct, :]), [r_const], [r_UUs])
                    flat = lambda t: t[:].rearrange("p a b -> p (a b)")
                    op("dve", lambda e: e.tensor_tensor_scan(out=flat(HHs), data0=flat(AAs), data1=flat(UUs),
                                                             initial=0.0, op0=ALU.mult, op1=ALU.add),
                       [r_AAs, r_UUs], [r_HHs])
                    op("dve", lambda e: e.scalar_tensor_tensor(
                        out=v3(yTL[bi % 2][:, ct, 0:n]), in0=HHs[:, :, 1:9], scalar=0.5,
                        in1=v3(SZ[k][:, 0:n]), op0=ALU.mult, op1=ALU.mult),
                       [r_HHs, r_SZ[k]], [r_yTL[bi % 2][ct]])
                    op("pool", lambda e: e.tensor_copy(out=hfin[:, ct, 1:17], in_=HHs[:, :, 8]), [r_HHs], [r_fin])

        s0_i = [0]

        def phase2(bi, kind):
            xTt, r_x = xT[bi % 2], r_xT[bi % 2]
            n = TB if kind == "p" else 128
            nch = n // 128
            PA2 = getattr(Sched, "PA2", True) and kind == "p"
            ring_ids[0] = [0, 1] if PA2 else [0, 1, 2]
            w, r_w = load_w([(5120, 16)])
            prg, r_prg = ring_next()
            proj(w, r_w, 0, 16, xTt, r_x, 0, n, prg[0:16, 0:n], r_prg)
            op("act", lambda e: e.activation(out=RGK[:, 0:n], in_=prg[0:16, 0:n], func=AF.Copy), [r_prg], [r_RGK])
            for h in range(4):
                wC, r_wC = load_w([(4096 + h * 256, 256)])
                for vh in range(2):
                    pzg, r_pzg = ring_next()
                    proj(wC, r_wC, vh * 128, 128, xTt, r_x, 0, n, pzg[:, 0:n], r_pzg)
                    tz, r_tz = TZ[vh], r_TZ[vh]
                    kk = 8 + 2 * h + vh
                    op("act", lambda e, pzg=pzg, tz=tz: e.activation(out=tz[:, 0:n], in_=pzg[:, 0:n], func=AF.Tanh, scale=0.5),
                       [r_pzg], [r_tz])
                    op("dve", lambda e, pzg=pzg, tz=tz, kk=kk: e.scalar_tensor_tensor(
                        out=yTG[:, kk - 8, 0:n], in0=tz[:, 0:n], scalar=1.0, in1=pzg[:, 0:n], op0=ALU.add, op1=ALU.mult),
                       [r_tz, r_pzg], [r_yTG[kk - 8]])
            for h in range(4):
                wA, r_wA = load_w([(2048 + h * 128, 128), (2560 + h * 128, 128)])
                wB, r_wB = load_w([(3072 + h * 256, 256)])
                if bi == 0:
                    load_wout(4)
                pg, r_pg = ring_next()
                op("pe", lambda e: e.matmul(pg[:, 0:n], lhsT=wgk2_bf[0:16, h * 128:(h + 1) * 128], rhs=RGK[0:16, 0:n],
                                            start=True, stop=True), [r_wgk2, r_RGK], [r_pg])
                EE, r_EE = EEb, r_EEb
                BB, r_BB = BBb, r_BBb
                E1, r_E1 = E123[0], r_E123[0]
                E2, r_E2 = E123[1], r_E123[1]
                E3, r_E3 = E123[2], r_E123[2]
                DD, r_DD = EEb, r_EEb
                op("act", lambda e: e.activation(out=EE[:, 0:n], in_=pg[:, 0:n], func=AF.Exp, scale=-1.0,
                                                 bias=dc(DC_NBGK + h)), [r_pg, r_dcols], [r_EE])
                op("act", lambda e: e.activation(out=EE[:, 0:n], in_=EE[:, 0:n], func=AF.Ln, bias=1.0), [r_EE], [r_EE])
                mcol = CS_CM if kind == "p" else CS_SM
                op("dve", lambda e: e.tensor_tensor_scan(out=BB[:, 0:n], data0=cst[:, mcol:mcol + n], data1=EE[:, 0:n],
                                                         initial=0.0, op0=ALU.mult, op1=ALU.add),
                   [r_EE, r_const], [r_BB])
                cl = 128 if kind == "p" else 8
                ncl = n // cl
                b3 = BB[:, 0:n].rearrange("p (a b) -> p a b", b=cl)
                op("pool", lambda e: e.tensor_tensor(out=DD[:, 0:n].rearrange("p (a b) -> p a b", b=cl), in0=b3,
                                                    in1=b3[:, :, cl - 1:cl].to_broadcast([128, ncl, cl]),
                                                    op=ALU.subtract), [r_BB], [r_DD])
                op("act", lambda e: e.activation(out=E1[:, 0:n], in_=BB[:, 0:n], func=AF.Exp, scale=-1.0 / 16),
                   [r_BB], [r_E1])
                op("act", lambda e: e.activation(out=E2[:, 0:n], in_=BB[:, 0:n], func=AF.Exp, scale=1.0 / 16),
                   [r_BB], [r_E2])
                op("act", lambda e: e.activation(out=E3[:, 0:n], in_=DD[:, 0:n], func=AF.Exp, scale=1.0 / 16),
                   [r_DD], [r_E3])
                op("act", lambda e: e.activation(out=DEC[:, 0:ncl], in_=b3[:, :, cl - 1], func=AF.Exp, scale=-1.0 / 16),
                   [r_BB], [r_DEC])
                pq, r_pq = ring_next()
                proj(wA, r_wA, 0, 128, xTt, r_x, 0, n, pq[:, 0:n], r_pq)
                op("dve", lambda e: e.scalar_tensor_tensor(out=QT[:, 0:n], in0=pq[:, 0:n], scalar=QSCALE, in1=E1[:, 0:n],
                                                           op0=ALU.mult, op1=ALU.mult), [r_pq, r_E1], [r_QT])
                pk, r_pk = ring_next()
                proj(wA, r_wA, 128, 128, xTt, r_x, 0, n, pk[:, 0:n], r_pk)
                op("dve", lambda e: e.tensor_tensor(out=KT[:, 0:n], in0=pk[:, 0:n], in1=E2[:, 0:n], op=ALU.mult),
                   [r_pk, r_E2], [r_KT])
                op("dve", lambda e: e.tensor_tensor(out=KE[:, 0:n], in0=pk[:, 0:n], in1=E3[:, 0:n], op=ALU.mult),
                   [r_pk, r_E3], [r_KE])
                for j in range(nch):
                    pv = pb_v[:, (j % 2) * 256:(j % 2 + 1) * 256]
                    for kc in range(8):
                        op("pe", lambda e, kc=kc, j=j, pv=pv: e.matmul(pv, lhsT=xTt[:, kc, j * 128:(j + 1) * 128],
                                                                      rhs=wB[:, kc, 0:256], start=(kc == 0), stop=(kc == 7)),
                           [r_wB, r_x], [r_pbv[j % 2]], signal=(kc == 7))
                    op("dve", lambda e, j=j, pv=pv: e.tensor_copy(out=VS[:, j, :], in_=pv),
                       [r_pbv[j % 2]], [r_VS[j]])
                for c in range(nch):
                    k = c % 2
                    pai = (4 if c % 2 == 0 else 2) if PA2 else 4
                    pba_c, r_pba_c = bank[pai], r_bank[pai]
                    cs = slice(c * 128, (c + 1) * 128)
                    op("pe", lambda e: e.transpose(out=ptr[:, 0:128], in_=KE[:, cs], identity=ident_bf[:]),
                       [r_KE, r_idb], [r_ptr])
                    op("dve", lambda e: e.tensor_copy(out=KEt[k][:], in_=ptr[:, 0:128]), [r_ptr], [r_KEt[k]])
                    op("pe", lambda e: e.matmul(pba_c[:, 0:128], lhsT=KT[:, cs], rhs=QT[:, cs], start=True, stop=True),
                       [r_KT, r_QT], [r_pba_c])
                    mk = CS_U if kind == "p" else CS_US
                    op("dve", lambda e: e.tensor_tensor(out=ATT[k][:], in0=pba_c[:, 0:128], in1=cst[:, mk:mk + 128],
                                                        op=ALU.mult), [r_pba_c, r_const], [r_ATT[k]])
                    if kind == "p":
                        pbi = 5 + (c % 2 if getattr(Sched, "PO2", True) else 0)
                        pbo = bank[pbi]
                        pos = (pbo[:, 0:128], pbo[:, 128:256])
                        r_pos = [r_bank[pbi], r_bank[pbi]]
                    else:
                        pos = (pb_o0[:, 0:128], pb_o1[:, 0:128])
                        r_pos = [r_bank[5], r_bank[6]]
                    if kind == "p":
                        for vh in range(2):
                            op("pe", lambda e, vh=vh: e.matmul(pos[vh], lhsT=VS[:, c, vh * 128:(vh + 1) * 128], rhs=ATT[k][:],
                                                               start=True, stop=False), [r_VS[c], r_ATT[k]], [r_pos[vh]],
                               signal=False)
                            op("pe", lambda e, vh=vh: e.matmul(pos[vh], lhsT=Sbf[:, h, vh * 128:(vh + 1) * 128], rhs=QT[:, cs],
                                                               start=False, stop=True), [r_Sbf[h], r_QT], [r_pos[vh]])
                        pu = pb_v[:, 0:256]
                        op("pe", lambda e: e.matmul(pu, lhsT=KEt[k][:], rhs=VS[:, c, :], start=True, stop=True),
                           [r_KEt[k], r_VS[c]], [r_pbv[0]])
                        op("dve", lambda e: e.scalar_tensor_tensor(out=Sst[:, h, :], in0=Sst[:, h, :], scalar=DEC[:, c:c + 1],
                                                                   in1=pu, op0=ALU.mult, op1=ALU.add),
                           [r_S[h], r_DEC, r_pbv[0]], [r_S[h]])
                        op("pool", lambda e: e.tensor_copy(out=Sbf[:, h, :], in_=Sst[:, h, :]), [r_S[h]], [r_Sbf[h]])
                        if last_p[0] and c == nch - 1:
                            dma("pool", os_d[0, h], Sst[:, h, :], [r_S[h]], [], "os_p", is_output=True)
                    else:
                        qm_out = bass.AP(QM.tensor, QM.offset, [[QM.ap[0][0], 128], [136, NSEQ], [1, 8]])
                        op("dve", lambda e: e.tensor_copy(out=qm_out, in_=QT[:, 0:128].rearrange("p (a b) -> p a b", b=8)),
                           [r_QT], [r_QM])
                        kem3 = KEM.rearrange("p (a b) -> p a b", b=128)
                        op("dve", lambda e: e.tensor_tensor(
                            out=kem3, in0=KEt[k][:].unsqueeze(1).to_broadcast([128, NSEQ, 128]),
                            in1=cst[:, CS_M:CS_M + NSEQ].unsqueeze(2).to_broadcast([128, NSEQ, 128]), op=ALU.mult),
                           [r_KEt[k], r_const], [r_KEM])
                        for vh in range(2):
                            op("pe", lambda e, vh=vh: e.matmul(pos[vh], lhsT=VS[:, c, vh * 128:(vh + 1) * 128], rhs=ATT[k][:],
                                                               start=True, stop=False), [r_VS[c], r_ATT[k]], [r_pos[vh]],
                               signal=False)
                        for i in range(NSEQ):
                            s = s0_i[0] % NS0
                            s0_i[0] += 1
                            dma("sp", S0[s], sgla_d[i, h], [], [r_S0[s]], f"S0{s}")
                            sbi = s % NSB
                            op("act", lambda e, s=s, sbi=sbi: e.activation(out=S0b[sbi][:], in_=S0[s], func=AF.Copy), [r_S0[s]], [r_S0b[sbi]])
                            for vh in range(2):
                                op("pe", lambda e, vh=vh, s=s, i=i, sbi=sbi: e.matmul(
                                    pos[vh], lhsT=S0b[sbi][:, vh * 128:(vh + 1) * 128], rhs=QM[:, i * 128:(i + 1) * 128],
                                    start=False, stop=(i == NSEQ - 1)), [r_S0b[sbi], r_QM], [r_pos[vh]],
                                   signal=(i == NSEQ - 1))
                            pub = [0, 1, 2, 3][i % 4]
                            pu = bank[pub][:, 0:256]
                            op("pe", lambda e, i=i, pu=pu: e.matmul(pu, lhsT=KEM[:, i * 128:(i + 1) * 128], rhs=VS[:, c, :],
                                                                    start=True, stop=True), [r_KEM, r_VS[c]], [r_bank[pub]])
                            op("dve", lambda e, s=s, i=i, pu=pu: e.scalar_tensor_tensor(
                                out=S0[s], in0=S0[s], scalar=DEC[:, i:i + 1], in1=pu, op0=ALU.mult, op1=ALU.add),
                               [r_S0[s], r_DEC, r_bank[pub]], [r_S0[s]])
                            dma("sp", os_d[1 + i, h], S0[s], [r_S0[s]], [], f"SNo{s}", is_output=True)
                    if kind == "p":
                        op("act", lambda e: e.activation(out=SQ[k][:], in_=pbo[:, 0:256], func=AF.Square),
                           [r_pos[0]], [r_SQ[k]])
                    else:
                        for vh in range(2):
                            op("act", lambda e, vh=vh: e.activation(out=SQ[k][:, vh * 128:(vh + 1) * 128], in_=pos[vh],
                                                                    func=AF.Square), [r_pos[vh]], [r_SQ[k]])
                    op("pe", lambda e: e.matmul(pba_c[:, 128:256], lhsT=ones_bf[:], rhs=SQ[k][:, 0:128], start=True, stop=False),
                       [r_idb, r_SQ[k]], [r_pba_c], signal=False)
                    op("pe", lambda e: e.matmul(pba_c[:, 128:256], lhsT=ones_bf[:], rhs=SQ[k][:, 128:256], start=False, stop=True),
                       [r_idb, r_SQ[k]], [r_pba_c])
                    op("act", lambda e: e.activation(out=RT[k][:], in_=pba_c[:, 128:256], func=AF.Ln, scale=1.0 / HV, bias=EPS),
                       [r_pba_c], [r_RT[k]])
                    op("act", lambda e: e.activation(out=RS[k][:], in_=RT[k][:], func=AF.Exp, scale=-0.5), [r_RT[k]], [r_RS[k]])
                    for vh in range(2):
                        kk = 8 + 2 * h + vh
                        op("dve", lambda e, vh=vh: e.scalar_tensor_tensor(out=Y1[vh][:], in0=pos[vh], scalar=dc(DC_GH2 + vh),
                                                                         in1=RS[k][:], op0=ALU.mult, op1=ALU.mult),
                           [r_pos[vh], r_RS[k], r_dcols], [r_Y1[vh]])
                        op("dve", lambda e, vh=vh, kk=kk: e.tensor_tensor(out=yTG[:, kk - 8, cs], in0=Y1[vh][:], in1=yTG[:, kk - 8, cs],
                                                                        op=ALU.mult), [r_Y1[vh], r_yTG[kk - 8]], [r_yTG[kk - 8]])

        xr_i = [0]

        def phase3(bi, row0, ntile):
            ring_ids[0] = list(getattr(Sched, "RING3", [7, 5, 6]))
            for j in range(ntile):
                xs_ = xi_i[0] % NXI
                xi_i[0] += 1
                s = 0
                rows = slice(row0 + j * 128, row0 + (j + 1) * 128)
                dma("sp", xin[xs_][:], x_d[rows, :], [], [r_xin[xs_]], f"xin{xs_}")
                pouts = []
                for half in range(2):
                    po_, r_po_ = ring_next()
                    for kc in range(16):
                        ysrc = yTL[bi % 2][:, kc, j * 128:(j + 1) * 128] if kc < 8 else yTG[:, kc - 8, j * 128:(j + 1) * 128]
                        r_ys = r_yTL[bi % 2][kc] if kc < 8 else r_yTG[kc - 8]
                        op("pe", lambda e, kc=kc, half=half, po_=po_, ysrc=ysrc: e.matmul(
                            po_[:], lhsT=ysrc, rhs=wout_bf[:, kc, half * 512:(half + 1) * 512],
                            start=(kc == 0), stop=(kc == 15)), [r_ys, r_wout_k[kc]], [r_po_], signal=(kc == 15))
                    op("act", lambda e, half=half, po_=po_: e.activation(out=junk[:], in_=po_[:],
                                                                        func=AF.Square, accum_out=sm(20 + half)),
                       [r_po_], [r_junk, rsm(20 + half)])
                    pouts.append((po_, r_po_))
                op("dve", lambda e: e.tensor_tensor(out=sm(22), in0=sm(20), in1=sm(21), op=ALU.add),
                   [rsm(20), rsm(21)], [rsm(22)])
                op("act", lambda e: e.activation(out=sm(23), in_=sm(22), func=AF.Ln, scale=1.0 / D, bias=EPS),
                   [rsm(22)], [rsm(23)])
                op("act", lambda e: e.activation(out=sm(24), in_=sm(23), func=AF.Exp, scale=-0.5), [rsm(23)], [rsm(24)])
                for half in range(2):
                    po_, r_po_ = pouts[half]
                    hs = slice(half * 512, (half + 1) * 512)
                    op("dve", lambda e, po_=po_, hs=hs, s=s: e.scalar_tensor_tensor(
                        out=osb[s][:, hs], in0=po_[:], scalar=sm(24), in1=gpost[:, hs], op0=ALU.mult, op1=ALU.mult),
                       [r_po_, rsm(24), r_const], [r_osb[s]])
                op("pool", lambda e, s=s, xs_=xs_: e.tensor_tensor(out=osb[s][:], in0=osb[s][:], in1=xin[xs_][:], op=ALU.add),
                   [r_osb[s], r_xin[xs_]], [r_osb[s]])
                dma("sp", y_d[rows, :], osb[s][:], [r_osb[s]], [], f"osb{s}", is_output=True)

        pblocks = [("p", b * TB, TB // 128) for b in range(NPB)]
        spos = getattr(Sched, "SPOS", 4)
        assert spos <= 4
        blocks = pblocks[:spos] + [("s", LP, 1)] + pblocks[spos:]
        phase0(0, blocks[0][1], blocks[0][2])
        late_setup()
        for bi, (kind, row0, ntile) in enumerate(blocks):
            first_pass[0] = (bi == 0)
            last_p[0] = (kind == "p" and row0 == LP - TB)
            S.tag = f"b{bi}"
            if kind == "s":
                op("pool", lambda e: e.memset(S0b[0][:, 0:1], 0.0), [], r_ax + [r_QM, r_KEM, r_S0b[0]] + r_S0)
                op("pool", lambda e: e.memset(QM, 0.0), [], [r_QM])
            phase1(bi, kind)
            phase2(bi, kind)
            if bi + 1 < len(blocks):
                phase0(bi + 1, blocks[bi + 1][1], blocks[bi + 1][2])
            phase3(bi, row0, ntile)
        dma("pool", oh_d, hfin[:].rearrange("p a b -> p (a b)"), [r_fin], [], "fin", is_output=True)
        dma("pool", oc_d, cfin[:].rearrange("p a b c -> p (a b c)"), [r_fin], [], "fin", is_output=True)
        S.finish()
        tg = {}
        for nd in S.nodes:
            t = tg.setdefault((nd["tag"], nd["eng"]), [1e9, 0.0, 0.0])
            t[0] = min(t[0], nd["start"]); t[1] = max(t[1], nd["fin"]); t[2] += nd["cost"]
        build_nc.stats = dict(nodes=len(S.nodes), nwait=S.nwait, est_us=S.est_total, tags=tg)
    return nc


_NC_CACHE = {}


def _consts():
    c = np.zeros((128, CS_N), np.float32)
    c[:, CS_ID:CS_ID + 128] = np.eye(128, dtype=np.float32)
    s = np.arange(128)[:, None]
    t = np.arange(128)[None, :]
    c[:, CS_U:CS_U + 128] = (s <= t)
    c[:, CS_US:CS_US + 128] = (s <= t) & (s // SL == t // SL)
    c[:, CS_M:CS_M + NSEQ] = (s // SL == np.arange(NSEQ)[None, :])
    cm = np.ones(TB, np.float32)
    cm[::128] = 0.0
    c[:, CS_CM:CS_CM + TB] = cm[None, :]
    smk = np.ones(128, np.float32)
    smk[::SL] = 0.0
    c[:, CS_SM:CS_SM + 128] = smk[None, :]
    return c


def make_in_maps(x_prompt, x_sample, state_lru_h, state_lru_conv, state_gla, g_pre, w_in, conv_w, conv_b,
           w_rg, b_rg, w_ig, b_ig, lru_lambda, w_gk2, b_gk, g_head, w_out, g_post):
    f = lambda a: np.ascontiguousarray(np.asarray(a, dtype=np.float32))
    x_prompt, x_sample = f(x_prompt), f(x_sample)
    col = lambda v, n: f(v).reshape(n, 128).T
    pcols = np.concatenate([
        col(g_pre[0], 8),
        np.concatenate([col(conv_w[0][j], 8) for j in range(4)], axis=1),
        col(conv_b[0], 8), col(b_rg[0], 8), col(b_ig[0], 8), col(lru_lambda[0], 8),
        col(b_gk[0], 4), col(g_head[0], 2)], axis=1)
    pcols = np.ascontiguousarray(pcols, dtype=np.float32)
    assert pcols.shape == (128, PC_N)
    wg = np.zeros((128, 2, 8, 128), np.float32)
    for gi, wsrc in enumerate((f(w_rg)[0], f(w_ig)[0])):
        for ct in range(8):
            wg[0:64, gi, ct, 0:64] = wsrc[2 * ct]
            wg[64:128, gi, ct, 64:128] = wsrc[2 * ct + 1]
    wg = wg.reshape(128, 2 * 8 * 128)
    gpost = np.ascontiguousarray(np.broadcast_to(f(g_post)[0][None, :], (128, D)))
    cst = _consts()
    win = f(w_in)[0]
    wout = f(w_out)[0]
    wgk2 = f(w_gk2)[0]
    in_maps = []
    for c in range(NCORES):
        sl = slice(c * NSEQ, (c + 1) * NSEQ)
        xs = np.concatenate([x_prompt[c], x_sample[sl].reshape(NSEQ * SL, D)], axis=0)
        shc = f(state_lru_conv)[0, sl]
        shc = shc.reshape(NSEQ, 3, 8, 128).transpose(3, 2, 0, 1).reshape(128, 8 * NSEQ * 3)
        shh = f(state_lru_h)[0, sl].reshape(NSEQ, 8, 128).transpose(2, 1, 0).reshape(128, 8 * NSEQ)
        in_maps.append({
            "x": np.ascontiguousarray(xs), "w_in": win, "w_out": wout, "wg": wg, "wgk2": wgk2, "pcols": pcols,
            "cst": cst, "gpost": gpost, "shc": np.ascontiguousarray(shc), "shh": np.ascontiguousarray(shh),
            "sgla": np.ascontiguousarray(f(state_gla)[0, sl]),
        })
    return in_maps


def kernel(**inputs):
    in_maps = make_in_maps(**inputs)
    if "nc" not in _NC_CACHE:
        _NC_CACHE["nc"] = build_nc()
    res = run_bass_kernel_spmd(_NC_CACHE["nc"], in_maps, core_ids=list(range(NCORES)))
    return assemble(res.results)


def assemble(R):
    y_p = np.stack([R[c]["y"][:LP] for c in range(NCORES)], 0)
    y_s = np.concatenate([R[c]["y"][LP:].reshape(NSEQ, SL, D) for c in range(NCORES)], 0)
    oh = [R[c]["o_h"].reshape(128, 8, 17).transpose(2, 1, 0).reshape(17, D) for c in range(NCORES)]
    oc = [R[c]["o_c"].reshape(128, 8, 17, 3).transpose(2, 3, 1, 0).reshape(17, 3, D) for c in range(NCORES)]
    h_p = np.stack([o[0] for o in oh], 0)[None]
    h_s = np.concatenate([o[1:] for o in oh], 0)[None]
    c_p = np.stack([o[0] for o in oc], 0)[None]
    c_s = np.concatenate([o[1:] for o in oc], 0)[None]
    s_p = np.stack([R[c]["o_s"][0] for c in range(NCORES)], 0)[None]
    s_s = np.concatenate([R[c]["o_s"][1:] for c in range(NCORES)], 0)[None]
    out = (y_p, y_s, h_p, c_p, s_p, h_s, c_s, s_s)
    return tuple(np.ascontiguousarray(o, dtype=np.float32) for o in out)
```

```python
import numpy as np
from contextlib import ExitStack
import concourse.bass as bass
import concourse.mybir as mybir
from concourse.bass_utils import run_bass_kernel_spmd

F32, BF16 = mybir.dt.float32, mybir.dt.bfloat16
ALU = mybir.AluOpType
AF = mybir.ActivationFunctionType

NCORES = 8
D = 1024
DIN = 5136
LP = 2048
NSEQ = 16
SL = 8
TB = 512
NPB = LP // TB
NTOK = LP + NSEQ * SL
EPS = 1e-6
HK = 128
HV = 256
QSCALE = float(HK) ** -0.5

PC_GPRE, PC_CW, PC_CB, PC_BRG, PC_BIG, PC_LAM, PC_BGK, PC_GH, PC_N = 0, 8, 40, 48, 56, 64, 72, 76, 78
DC_HBRG, DC_HBIG, DC_CH, DC_CF, DC_NBGK, DC_GH2, DC_N = 0, 8, 16, 24, 32, 36, 40
CS_ID, CS_U, CS_US, CS_M, CS_CM, CS_SM, CS_N = 0, 128, 256, 384, 400, 912, 1040


class Res:
    __slots__ = ("name", "w", "r")

    def __init__(self, name):
        self.name = name
        self.w = None
        self.r = []


class _Rec:
    def __init__(self):
        self.calls = []

    def __getattr__(self, name):
        def f(*args, **kwargs):
            self.calls.append((name, args, kwargs))
            return self
        return f


def _ap_elems(ap):
    n = 1
    for d in ap.shape[1:]:
        n *= d
    return n


class Sched:
    ENGS = ("pe", "act", "dve", "pool", "sp")

    def __init__(self, nc, es):
        self.nc = nc
        self.es = es
        self.E = {"pe": nc.tensor, "act": nc.scalar, "dve": nc.vector, "pool": nc.gpsimd, "sp": nc.sync}
        self.nodes = []
        self.pend_pe = None
        self.out_dma_nodes = []
        self.tag = ""

    def _collect(self, reads, writes):
        deps = {}
        for r in reads:
            if r.w is not None:
                deps[r.w] = True
        for w in writes:
            if w.w is not None:
                deps.setdefault(w.w, False)
            for t in w.r:
                deps.setdefault(t, False)
        return deps

    def _cost(self, eng, name, kwargs):
        try:
            if eng == "pe":
                if name == "transpose":
                    return 0.11
                n = _ap_elems(kwargs["rhs"])
                return 0.06 + n * 0.00039
            out = kwargs.get("out", None)
            if out is None:
                out = kwargs.get("ap", None)
            n = _ap_elems(out) if out is not None else 512
            if eng == "act":
                return 0.22 + n * 0.00085
            if eng == "dve":
                if name == "tensor_tensor_scan":
                    return 0.25 + n * 0.0021
                return 0.15 + n * 0.00105
            if eng == "pool":
                return 0.2 + n * 0.0032
        except Exception:
            pass
        return 0.5

    def op(self, eng, fn, reads=(), writes=(), signal=True):
        rec = _Rec()
        fn(rec)
        assert len(rec.calls) == 1
        name, args, kwargs = rec.calls[0]
        deps = self._collect(reads, writes)
        cost = self._cost(eng, name, kwargs)
        if eng == "pe" and self.pend_pe is not None:
            node = self.pend_pe
            node["calls"].append((name, args, kwargs))
            for d, raw in deps.items():
                if d != node["id"]:
                    node["deps"][d] = node["deps"].get(d, False) or raw
            node["cost"] += cost
        else:
            node = {"id": len(self.nodes), "eng": eng, "calls": [(name, args, kwargs)], "deps": dict(deps),
                    "cost": cost, "dma": False, "tset": None, "tag": self.tag}
            if eng == "act" and name == "activation":
                f = kwargs.get("func")
                node["tset"] = {AF.Tanh: "T", AF.Ln: "L", AF.Sqrt: "S"}.get(f, None)
            self.nodes.append(node)
        nid = node["id"]
        if not signal:
            assert eng == "pe"
            self.pend_pe = node
        else:
            self.pend_pe = None
        for r in reads:
            if nid not in r.r:
                r.r.append(nid)
        for w in writes:
            w.w = nid
            w.r = []
        return nid

    def dma(self, q, out, in_, reads, writes, semkey, is_output=False):
        assert self.pend_pe is None or q != "pe"
        deps = self._collect(reads, writes)
        nbytes = 128 * _ap_elems(out) * (2 if out.dtype == BF16 else 4)
        node = {"id": len(self.nodes), "eng": q, "calls": [("dma_start", (), {"out": out, "in_": in_})],
                "deps": dict(deps), "cost": 0.12 if q == "sp" else 1.0, "dma": True, "semkey": semkey, "bytes": nbytes,
                "tag": self.tag}
        self.nodes.append(node)
        nid = node["id"]
        if is_output:
            self.out_dma_nodes.append(nid)
        for r in reads:
            if nid not in r.r:
                r.r.append(nid)
        for w in writes:
            w.w = nid
            w.r = []
        return nid

    def _schedule(self):
        nodes = self.nodes
        n = len(nodes)
        nsucc = [[] for _ in range(n)]
        indeg = [0] * n
        for nd in nodes:
            for d in nd["deps"]:
                nsucc[d].append(nd["id"])
                indeg[nd["id"]] += 1
        last_sem = {}
        for nd in nodes:
            if nd["dma"]:
                k = nd["semkey"]
                if k in last_sem and last_sem[k] not in nd["deps"]:
                    nd["deps"][last_sem[k]] = False
                    nsucc[last_sem[k]].append(nd["id"])
                    indeg[nd["id"]] += 1
                last_sem[k] = nd["id"]
        bl = [0.0] * n
        for nd in reversed(nodes):
            i = nd["id"]
            c = (2.0 + nd["bytes"] / 240e3) if nd["dma"] else nd["cost"] + 0.15
            m = 0.0
            for j in nsucc[i]:
                if bl[j] > m:
                    m = bl[j]
            bl[i] = c + m
        self.bl = bl
        PRI_WIN = getattr(Sched, "PRI_WIN", 0.3)
        finish = [0.0] * n
        ready_t = [0.0] * n
        free_at = {e: 0.0 for e in self.ENGS}
        pipe_t = 0.0
        import heapq
        ready = {e: [] for e in self.ENGS}
        for nd in nodes:
            if indeg[nd["id"]] == 0:
                heapq.heappush(ready[nd["eng"]], (0.0, nd["id"]))
        order = []
        done = 0
        cur_set = [None]
        SWITCH = getattr(Sched, "SWITCH", 1.4)
        ACT_WIN = getattr(Sched, "ACT_WIN", 2.5)
        while done < n:
            best = None
            for e in self.ENGS:
                h = ready[e]
                if not h:
                    continue
                base = max(free_at[e], h[0][0])
                lim = base + (ACT_WIN if e == "act" else PRI_WIN + 1e-9)
                cand = None
                for (rt, i) in h:
                    if rt > lim:
                        continue
                    pen = 0.0
                    if e == "act":
                        ts_ = nodes[i]["tset"]
                        if ts_ is not None and cur_set[0] is not None and ts_ != cur_set[0]:
                            pen = SWITCH
                    stt = max(free_at[e], rt) + pen
                    if PRI_WIN > 0:
                        key = (round(stt / PRI_WIN) if e != "act" else stt, -bl[i], i)
                    else:
                        key = (stt, i)
                    if cand is None or key < cand[0]:
                        cand = (key, rt, i, pen, stt)
                st = cand[4] if PRI_WIN > 0 else cand[0][0]
                if best is None or st < best[0] or (st == best[0] and cand[2] < best[2]):
                    best = (st, e, cand[2], cand[1])
            st, e, i, rt = best
            ready[e].remove((rt, i))
            heapq.heapify(ready[e])
            nd = nodes[i]
            if e == "act" and nd["tset"] is not None:
                cur_set[0] = nd["tset"]
            if nd["dma"]:
                free_at[e] = st + nd["cost"]
                pipe_t = max(pipe_t, st) + nd["bytes"] / getattr(Sched, "DMA_BPUS", 240e3)
                finish[i] = max(st + 2.0, pipe_t + 1.5)
            else:
                free_at[e] = st + nd["cost"]
                finish[i] = free_at[e] + 0.15
            nd["start"] = st
            nd["fin"] = finish[i]
            order.append(i)
            done += 1
            for j in nsucc[i]:
                indeg[j] -= 1
                if finish[i] > ready_t[j]:
                    ready_t[j] = finish[i]
                if indeg[j] == 0:
                    heapq.heappush(ready[nodes[j]["eng"]], (ready_t[j], j))
        self.est_total = max(finish) if finish else 0.0
        return order

    def finish(self):
        assert self.pend_pe is None
        order = self._schedule()
        nc, es = self.nc, self.es
        sems, cnt, hist, dcnt = {}, {}, {}, {}
        seen = {k: {} for k in self.ENGS}
        for k in ("pe", "act", "dve", "pool"):
            sems[k] = es.enter_context(nc.semaphore("s_" + k))
            cnt[k] = 0
            hist[k] = [{}]
        tok = {}
        self.nwait = 0

        def wait(eng, t):
            key, val = t
            if seen[eng].get(key, 0) >= val:
                return
            self.E[eng].wait_ge(sems[key], val)
            self.nwait += 1
            seen[eng][key] = val
            if key in hist and val < len(hist[key]):
                for k2, v2 in hist[key][val].items():
                    if seen[eng].get(k2, 0) < v2:
                        seen[eng][k2] = v2

        EMBED = getattr(Sched, "EMBED", True)

        def mark_seen(eng, t):
            key, val = t
            seen[eng][key] = val
            if key in hist and val < len(hist[key]):
                for k2, v2 in hist[key][val].items():
                    if seen[eng].get(k2, 0) < v2:
                        seen[eng][k2] = v2

        for i in order:
            nd = self.nodes[i]
            eng = nd["eng"]
            need = {}
            for d, raw in nd["deps"].items():
                t = tok[d]
                if need.get(t[0], 0) < t[1]:
                    need[t[0]] = t[1]
            items = sorted(need.items(), key=lambda kv: (kv[0] == eng, kv[0] not in hist, kv[0]))
            snap = dict(seen[eng])
            pend = []
            for k, v in items:
                if seen[eng].get(k, 0) >= v:
                    continue
                pend.append((k, v))
                mark_seen(eng, (k, v))
            seen[eng] = snap
            emb = None
            multi = any(("accum_out" in c[2] and c[2]["accum_out"] is not None) for c in nd["calls"][:1])
            if EMBED and pend and not nd["dma"] and not multi:
                emb = pend.pop()
            for k, v in pend:
                wait(eng, (k, v))
            inst = None
            for ci, (name, args, kwargs) in enumerate(nd["calls"]):
                inst = getattr(self.E[eng], name)(*args, **kwargs)
                if ci == 0 and emb is not None:
                    inst._wait_ge(sems[emb[0]], emb[1])
                    mark_seen(eng, emb)
            if nd["dma"]:
                key = nd["semkey"]
                if key not in sems:
                    sems[key] = es.enter_context(nc.semaphore("d_" + key))
                    dcnt[key] = 0
                inst.then_inc(sems[key], 16)
                dcnt[key] += 16
                tok[i] = (key, dcnt[key])
            else:
                inst.then_inc(sems[eng], 1)
                cnt[eng] += 1
                hist[eng].append(dict(seen[eng]))
                tok[i] = (eng, cnt[eng])
        for i in self.out_dma_nodes:
            wait("sp", tok[i])
        for k in ("pe", "act", "dve", "pool"):
            wait("sp", (k, cnt[k]))


def build_nc():
    nc = bass.Bass("TRN2", target_bir_lowering=False)
    dt_in = lambda name, shape: nc.dram_tensor(name, list(shape), F32, kind="ExternalInput").ap()
    dt_out = lambda name, shape: nc.dram_tensor(name, list(shape), F32, kind="ExternalOutput").ap()
    x_d = dt_in("x", [NTOK, D])
    win_d = dt_in("w_in", [D, DIN])
    wout_d = dt_in("w_out", [2 * D, D])
    wg_d = dt_in("wg", [128, 2 * 8 * 128])
    wgk2_d = dt_in("wgk2", [16, 512])
    pcols_d = dt_in("pcols", [128, PC_N])
    cst_d = dt_in("cst", [128, CS_N])
    gpost_d = dt_in("gpost", [128, D])
    shc_d = dt_in("shc", [128, 8 * NSEQ * 3])
    shh_d = dt_in("shh", [128, 8 * NSEQ])
    sgla_d = dt_in("sgla", [NSEQ, 4, 128, 256])

    y_d = dt_out("y", [NTOK, D])
    oh_d = dt_out("o_h", [128, 8 * 17])
    oc_d = dt_out("o_c", [128, 8 * 17 * 3])
    os_d = dt_out("o_s", [17, 4, 128, 256])
    wbf_d = nc.dram_tensor("wbf_scratch", [128, 41, 8, 128], BF16).ap()
    win_v = win_d.rearrange("(kc p) n -> p kc n", p=128)
    wout_v = wout_d.rearrange("(kc p) n -> p kc n", p=128)

    with ExitStack() as es:
        S = Sched(nc, es)
        op, dma = S.op, S.dma

        def sb(name, shape, dtype):
            return es.enter_context(nc.sbuf_tensor("sb_" + name, list(shape), dtype))

        def ps(name, shape, dtype):
            return es.enter_context(nc.psum_tensor("ps_" + name, list(shape), dtype))

        xT = [sb(f"xT{i}", [128, 8, TB], BF16) for i in range(2)]
        r_xT = [Res(f"xT{i}") for i in range(2)]
        yTL = [sb(f"yTL{i}", [128, 8, TB], BF16) for i in range(2)]
        r_yTL = [[Res(f"yTL{i}_{k}") for k in range(8)] for i in range(2)]
        yTG = sb("yTG", [128, 8, TB], BF16)
        r_yTG = [Res(f"yTG{k}") for k in range(8)]
        wout_bf = sb("wout_bf", [128, 16, D], BF16)
        r_wout_k = [Res(f"wout{g}") for g in range(16)]
        NWL = getattr(Sched, "NWL", 4)
        wl = [sb(f"wl{i}", [128, 8, 256], BF16) for i in range(NWL)]
        r_wl = [Res(f"wl{i}") for i in range(NWL)]
        NXI = 2
        xin = [sb(f"xin{i}", [128, D], F32) for i in range(NXI)]
        r_xin = [Res(f"xin{i}") for i in range(NXI)]
        osb = [sb(f"osb{i}", [128, D], F32) for i in range(1)]
        r_osb = [Res(f"osb{i}") for i in range(1)]
        xsb = [sb(f"xsb{i}", [128, D], BF16) for i in range(2)]
        r_xsb = [Res(f"xsb{i}") for i in range(2)]
        junk = sb("junk", [128, 512], BF16)
        r_junk = Res("junk")
        small = sb("small", [128, 64], F32)
        pcols = sb("pcols", [128, PC_N], F32)
        dcols = sb("dcols", [128, DC_N], F32)
        cst = sb("cst", [128, CS_N], F32)
        gpost = sb("gpost", [128, D], F32)
        wg_bf = sb("wg_bf", [128, 2 * 8 * 128], BF16)
        wgk2_bf = sb("wgk2_bf", [16, 512], BF16)
        ident_bf = sb("ident_bf", [128, 128], BF16)
        ones_bf = sb("ones_bf", [128, 128], BF16)
        shc = sb("shc", [128, 8, NSEQ, 3], F32)
        shh = sb("shh", [128, 8, NSEQ], F32)
        hst = sb("hst", [128, 8], F32)
        cst3 = sb("cst3", [128, 8, 3], F32)
        hfin = sb("hfin", [128, 8, 17], F32)
        cfin = sb("cfin", [128, 8, 17, 3], F32)
        Sst = sb("Sst", [128, 4, HV], F32)
        Sbf = sb("Sbf", [128, 4, HV], BF16)
        r_const = Res("const")
        r_dcols = Res("dcols")
        r_wgbf = Res("wgbf")
        r_wgk2 = Res("wgk2bf")
        r_idb = Res("identbf")
        r_hst = [Res(f"hst{c}") for c in range(8)]
        r_cst3 = [Res(f"cst3{c}") for c in range(8)]
        r_fin = Res("fin")
        r_S = [Res(f"S{h}") for h in range(4)]
        r_Sbf = [Res(f"Sbf{h}") for h in range(4)]

        def tmpf(name, n=TB, k=2, dtype=F32):
            return [sb(f"{name}{i}", [128, n], dtype) for i in range(k)], [Res(f"{name}{i}") for i in range(k)]

        XL, r_XL = tmpf("XL", TB + 4, 2, F32)
        TR, r_TR = tmpf("TR")
        TI, r_TI = tmpf("TI")
        A2, r_A2 = tmpf("A2")
        AA, r_AA = tmpf("AA")
        UU, r_UU = tmpf("UU")
        HH, r_HH = tmpf("HH")
        TZ, r_TZ = tmpf("TZ", TB, 2, BF16)
        XCB, r_XCB = tmpf("XCB", TB, 2, BF16)
        SZ, r_SZ = tmpf("SZ", TB, 2, BF16)
        EEb = sb("EEb", [128, TB], F32)
        BBb = sb("BBb", [128, TB], F32)
        E123 = [sb(f"E{i}b", [128, TB], BF16) for i in range(3)]
        r_EEb, r_BBb, r_E123 = Res("EEb"), Res("BBb"), [Res(f"E{i}b") for i in range(3)]
        XC, r_XC = tmpf("XC")
        XLs = sb("XLs", [128, NSEQ, 11], F32)
        AAs = sb("AAs", [128, NSEQ, 9], F32)
        UUs = sb("UUs", [128, NSEQ, 9], F32)
        HHs = sb("HHs", [128, NSEQ, 9], F32)
        r_XLs, r_AAs, r_UUs, r_HHs = Res("XLs"), Res("AAs"), Res("UUs"), Res("HHs")
        RGK = sb("RGK", [16, TB], BF16)
        r_RGK = Res("RGK")
        QT = sb("QT", [128, TB], BF16)
        KT = sb("KT", [128, TB], BF16)
        KE = sb("KE", [128, TB], BF16)
        r_QT, r_KT, r_KE = Res("QT"), Res("KT"), Res("KE")
        VS = sb("VS", [128, 4, HV], BF16)
        r_VS = [Res(f"VS{j}") for j in range(4)]
        DEC = sb("DEC", [128, 16], F32)
        r_DEC = Res("DEC")
        KEt = [sb(f"KEt{i}", [128, 128], BF16) for i in range(2)]
        r_KEt = [Res(f"KEt{i}") for i in range(2)]
        ATT = [sb(f"ATT{i}", [128, 128], BF16) for i in range(2)]
        r_ATT = [Res(f"ATT{i}") for i in range(2)]
        SQ = [sb(f"SQ{i}", [128, 256], BF16) for i in range(2)]
        r_SQ = [Res(f"SQ{i}") for i in range(2)]
        RT = [sb(f"RT{i}", [128, 128], F32) for i in range(2)]
        r_RT = [Res(f"RT{i}") for i in range(2)]
        RS = [sb(f"RS{i}", [128, 128], F32) for i in range(2)]
        r_RS = [Res(f"RS{i}") for i in range(2)]
        Y1 = [sb(f"Y1{i}", [128, 128], F32) for i in range(2)]
        r_Y1 = [Res(f"Y1{i}") for i in range(2)]
        arena = sb("arena", [128, 4096], F32)
        r_ax = [Res(f"ax{i}") for i in range(4)]
        QM = arena[:, 0:1024].bitcast(BF16)
        KEM = arena[:, 1024:2048].bitcast(BF16)
        r_QM, r_KEM = Res("QM"), Res("KEM")
        NS0 = 8
        S0 = [arena[:, 2048 + i * 256:2048 + (i + 1) * 256] for i in range(NS0)]
        r_S0 = [Res(f"S0{i}") for i in range(NS0)]
        NSB = 4
        S0b = [sb(f"S0b{i}", [128, HV], BF16) for i in range(NSB)]
        r_S0b = [Res(f"S0b{i}") for i in range(NSB)]

        bank = [ps(f"pb{i}", [128, 512], F32) for i in range(7)]
        r_bank = [Res(f"pb{i}") for i in range(7)]
        ptr = ps("ptr", [128, 1024], BF16)
        r_ptr = Res("ptr")
        bank.append(ptr[:].bitcast(F32))
        r_bank.append(r_ptr)
        ring_ids = [list(range(7))]
        ring_i = [0]

        def ring_next():
            ids = ring_ids[0]
            i = ids[ring_i[0] % len(ids)]
            ring_i[0] += 1
            return bank[i], r_bank[i]

        pb_v, pb_a, pb_o0, pb_o1 = bank[3], bank[4], bank[5], bank[6]
        r_pbv = [r_bank[3], r_bank[3]]
        r_patt = r_bank[4]
        r_pss = r_bank[4]
        r_po = [r_bank[5], r_bank[6]]

        pc = lambda c: pcols[:, c:c + 1]
        dc = lambda c: dcols[:, c:c + 1]
        sm = lambda c, n=1: small[:, c:c + n]
        r_sm = {}

        def rsm(c):
            if c not in r_sm:
                r_sm[c] = Res(f"small{c}")
            return r_sm[c]

        r_cparts = []
        for (dst, src) in ((pcols[:], pcols_d), (cst[:], cst_d), (gpost[:], gpost_d),                            (shc[:].rearrange("p a b c -> p (a b c)"), shc_d), (shh[:].rearrange("p a b -> p (a b)"), shh_d)):
            rc_ = Res("c_" + str(len(r_cparts)))
            r_cparts.append(rc_)
            dma("sp", dst, src, [], [rc_], f"const{len(r_cparts)}")
        op("pool", lambda e: e.memset(small[:, 63:64], 0.0), r_cparts, [r_const])
        dma("pool", ident_bf[:], cst_d[:, CS_ID:CS_ID + 128], [], [r_idb], "identq")

        def late_setup():
          dma("pool", wg_bf[:], wg_d, [], [r_wgbf], "wgbf")
          dma("pool", wgk2_bf[:], wgk2_d, [], [r_wgk2], "wgk2")
          op("dve", lambda e: e.memset(ones_bf[:], 1.0), [], [r_idb])
          op("dve", lambda e: e.memset(hst[:], 0.0), [], r_hst)
          op("dve", lambda e: e.memset(cst3[:].rearrange("p a b -> p (a b)"), 0.0), [], r_cst3)
          op("dve", lambda e: e.memset(Sst[:].rearrange("p a b -> p (a b)"), 0.0), [], r_S)
          op("dve", lambda e: e.memset(Sbf[:].rearrange("p a b -> p (a b)"), 0.0), [], r_Sbf)
          op("dve", lambda e: e.memset(AAs[:].rearrange("p a b -> p (a b)"), 0.0), [], [r_AAs])
          op("dve", lambda e: e.memset(UUs[:].rearrange("p a b -> p (a b)"), 0.0), [], [r_UUs])
          op("dve", lambda e: e.tensor_scalar(out=dcols[:, DC_HBRG:DC_HBRG + 16], in0=pcols[:, PC_BRG:PC_BRG + 16],
                                              scalar1=0.5, scalar2=None, op0=ALU.mult), [r_const], [r_dcols])
          op("dve", lambda e: e.tensor_scalar(out=dcols[:, DC_NBGK:DC_NBGK + 4], in0=pcols[:, PC_BGK:PC_BGK + 4],
                                              scalar1=-1.0, scalar2=None, op0=ALU.mult), [r_const], [r_dcols])
          op("dve", lambda e: e.tensor_scalar(out=dcols[:, DC_GH2:DC_GH2 + 2], in0=pcols[:, PC_GH:PC_GH + 2],
                                              scalar1=0.5, scalar2=None, op0=ALU.mult), [r_const], [r_dcols])
          op("act", lambda e: e.activation(out=sm(0, 8), in_=pcols[:, PC_LAM:PC_LAM + 8], func=AF.Exp, scale=-1.0),
             [r_const], [rsm(0)])
          op("act", lambda e: e.activation(out=sm(8, 8), in_=sm(0, 8), func=AF.Ln, bias=1.0), [rsm(0)], [rsm(8)])
          op("dve", lambda e: e.tensor_scalar(out=dcols[:, DC_CH:DC_CH + 8], in0=sm(8, 8), scalar1=-4.0, scalar2=None,
                                              op0=ALU.mult), [rsm(8)], [r_dcols])
          op("dve", lambda e: e.tensor_scalar(out=dcols[:, DC_CF:DC_CF + 8], in0=sm(8, 8), scalar1=-8.0, scalar2=None,
                                              op0=ALU.mult), [rsm(8)], [r_dcols])

        r_wd = {}
        gpre_bc = pcols[:, PC_GPRE:PC_GPRE + 8].unsqueeze(2).to_broadcast([128, 8, 128])
        wl_i = [0]
        last_p = [False]
        first_pass = [True]

        def load_w(parts):
            s = wl_i[0] % NWL
            wl_i[0] += 1
            off = 0
            for (c0, n) in parts:
                for tt in range((n + 127) // 128):
                    t = c0 // 128 + tt
                    nn = min(128, n - tt * 128)
                    if first_pass[0]:
                        cc = c0 + tt * 128
                        dma("pool", wl[s][:, :, off:off + nn], win_v[:, :, cc:cc + nn], [], [r_wl[s]], f"wlq{s}")
                        r_wd[t] = Res(f"wd{t}")
                        dma("sp", wbf_d[:, t, :, 0:nn], wl[s][:, :, off:off + nn], [r_wl[s]], [r_wd[t]], f"wds{s}")
                    else:
                        dma("sp", wl[s][:, :, off:off + nn], wbf_d[:, t, :, 0:nn], [r_wd[t]], [r_wl[s]], f"wl{s}")
                    off += nn
            return wl[s], r_wl[s]

        wo_i = [0]

        def load_wout(k):
            for _ in range(k):
                g = wo_i[0]
                if g >= 16:
                    return
                wo_i[0] += 1
                dma("pool", wout_bf[:, g, :], wout_v[:, g, :], [], [r_wout_k[g]], f"wout{g % 4}")

        def proj(w, r_w, woff, m, xTt, r_x, c0, n, out, r_out):
            for kc in range(8):
                op("pe", lambda e, kc=kc: e.matmul(out, lhsT=w[:, kc, woff:woff + m], rhs=xTt[:, kc, c0:c0 + n],
                                                    start=(kc == 0), stop=(kc == 7)),
                   [r_w, r_x], [r_out], signal=(kc == 7))

        xi_i = [0]

        def phase0(bi, row0, ntile):
            xTt, r_x = xT[bi % 2], r_xT[bi % 2]
            for j in range(ntile):
                s = xi_i[0] % NXI
                xi_i[0] += 1
                s2 = s % 2
                rows = slice(row0 + j * 128, row0 + (j + 1) * 128)
                if bi == 0:
                    xsrc, r_xs = arena[:, j * 1024:(j + 1) * 1024], r_ax[j]
                    dma("sp", xsrc, x_d[rows, :], [], [r_xs], f"ax{j}")
                else:
                    xsrc, r_xs = xin[s][:], r_xin[s]
                    dma("sp", xsrc, x_d[rows, :], [], [r_xs], f"xin{s}")
                op("dve", lambda e, xsrc=xsrc, s2=s2: e.scalar_tensor_tensor(out=xsb[s2][:], in0=xsrc, scalar=1.0, in1=xsrc,
                                                                        op0=ALU.mult, op1=ALU.mult, accum_out=sm(16)),
                   [r_xs], [r_xsb[s2], rsm(16)])
                op("act", lambda e: e.activation(out=sm(17), in_=sm(16), func=AF.Ln, scale=1.0 / D, bias=EPS),
                   [rsm(16)], [rsm(17)])
                op("act", lambda e: e.activation(out=sm(18), in_=sm(17), func=AF.Exp, scale=-0.5), [rsm(17)], [rsm(18)])
                op("dve", lambda e, xsrc=xsrc, s2=s2: e.tensor_scalar(out=xsb[s2][:], in0=xsrc, scalar1=sm(18),
                                                                 scalar2=None, op0=ALU.mult),
                   [r_xs, rsm(18)], [r_xsb[s2]])
                for kc in range(8):
                    op("pe", lambda e, kc=kc, s2=s2: e.transpose(out=ptr[:, kc * 128:(kc + 1) * 128],
                                                                  in_=xsb[s2][:, kc * 128:(kc + 1) * 128],
                                                                  identity=ident_bf[:]),
                       [r_xsb[s2], r_idb], [r_ptr], signal=(kc == 7))
                op("dve", lambda e, j=j: e.tensor_tensor(out=xTt[:, :, j * 128:(j + 1) * 128],
                                                        in0=ptr[:].rearrange("p (a b) -> p a b", a=8), in1=gpre_bc,
                                                        op=ALU.mult), [r_ptr, r_const], [r_x])

        def phase1(bi, kind):
            xTt, r_x = xT[bi % 2], r_xT[bi % 2]
            n = TB if kind == "p" else 128
            ring_ids[0] = list(getattr(Sched, "RING1", [0, 1, 2, 3, 4]))
            v3 = lambda ap: ap.rearrange("p (a b) -> p a b", b=8)
            for ct in range(8):
                k = ct % 2
                w, r_w = load_w([(ct * 128, 128), (1024 + ct * 128, 128)])
                px, r_px = ring_next()
                proj(w, r_w, 0, 128, xTt, r_x, 0, n, px[:, 0:n], r_px)
                pz, r_pz = ring_next()
                proj(w, r_w, 128, 128, xTt, r_x, 0, n, pz[:, 0:n], r_pz)
                if kind == "p":
                    xl, r_xl = XL[k], r_XL[k]
                    op("pool", lambda e: e.tensor_copy(out=xl[:, 0:3], in_=cst3[:, ct, :]), [r_cst3[ct]], [r_xl])
                    op("act", lambda e: e.activation(out=xl[:, 3:3 + n], in_=px[:, 0:n], func=AF.Copy), [r_px], [r_xl])
                    op("pool", lambda e: e.tensor_copy(out=cst3[:, ct, :], in_=xl[:, n:n + 3]), [r_xl], [r_cst3[ct]])
                    if last_p[0]:
                        op("act", lambda e: e.activation(out=cfin[:, ct, 0, :], in_=px[:, n - 3:n], func=AF.Copy),
                           [r_px], [r_fin])
                    taps = [xl[:, j:j + n] for j in range(4)]
                else:
                    xl, r_xl = XLs, r_XLs
                    op("pool", lambda e: e.tensor_copy(out=XLs[:, :, 0:3], in_=shc[:, ct, :, :]), [r_const], [r_xl])
                    op("act", lambda e: e.activation(out=XLs[:, :, 3:11], in_=v3(px[:, 0:n]), func=AF.Copy), [r_px], [r_xl])
                    op("act", lambda e: e.activation(out=cfin[:, ct, 1:17, :], in_=v3(px[:, 0:n])[:, :, 5:8], func=AF.Copy),
                       [r_px], [r_fin])
                    taps = [XLs[:, :, j:j + 8] for j in range(4)]
                xc_v = XC[k][:, 0:n] if kind == "p" else v3(XC[k][:, 0:n])
                pcv, r_pc = XC[k], r_XC[k]
                op("act", lambda e: e.activation(out=XC[k][:, 0:n], in_=px[:, 0:n], func=AF.Identity,
                                                 scale=pc(PC_CW + 3 * 8 + ct), bias=pc(PC_CB + ct)), [r_px, r_const], [r_XC[k]])
                for j in range(3):
                    wj = pc(PC_CW + j * 8 + ct)
                    op("dve", lambda e, j=j, wj=wj: e.scalar_tensor_tensor(out=xc_v, in0=taps[j], scalar=wj, in1=xc_v,
                                                                       op0=ALU.mult, op1=ALU.add),
                       [r_xl, r_XC[k], r_const], [r_XC[k]])
                op("dve", lambda e: e.tensor_copy(out=XCB[k][:, 0:n], in_=XC[k][:, 0:n]), [r_XC[k]], [r_XCB[k]])
                op("act", lambda e: e.activation(out=TZ[k][:, 0:n], in_=pz[:, 0:n], func=AF.Tanh, scale=0.5),
                   [r_pz], [r_TZ[k]])
                op("dve", lambda e: e.scalar_tensor_tensor(out=SZ[k][:, 0:n], in0=TZ[k][:, 0:n], scalar=1.0,
                                                           in1=pz[:, 0:n], op0=ALU.add, op1=ALU.mult),
                   [r_TZ[k], r_pz], [r_SZ[k]])
                pr_, r_pr = ring_next()
                op("pe", lambda e: e.matmul(pr_[:, 0:n], lhsT=wg_bf[:, ct * 128:(ct + 1) * 128], rhs=XCB[k][:, 0:n],
                                            start=True, stop=True), [r_wgbf, r_XCB[k]], [r_pr])
                pi_, r_pi = ring_next()
                op("pe", lambda e: e.matmul(pi_[:, 0:n], lhsT=wg_bf[:, (8 + ct) * 128:(9 + ct) * 128], rhs=XCB[k][:, 0:n],
                                            start=True, stop=True), [r_wgbf, r_XCB[k]], [r_pi])
                op("act", lambda e: e.activation(out=TR[k][:, 0:n], in_=pr_[:, 0:n], func=AF.Tanh, scale=0.5,
                                                 bias=dc(DC_HBRG + ct)), [r_pr, r_dcols], [r_TR[k]])
                op("act", lambda e: e.activation(out=TI[k][:, 0:n], in_=pi_[:, 0:n], func=AF.Tanh, scale=0.5,
                                                 bias=dc(DC_HBIG + ct)), [r_pi, r_dcols], [r_TI[k]])
                op("dve", lambda e: e.scalar_tensor_tensor(out=TI[k][:, 0:n], in0=TI[k][:, 0:n], scalar=1.0,
                                                           in1=pcv[:, 0:n], op0=ALU.add, op1=ALU.mult),
                   [r_TI[k], r_pc], [r_TI[k]])
                if kind == "p":
                    a_out, r_a = AA[k][:, 0:n], r_AA[k]
                    u_out, r_u = UU[k][:, 0:n], r_UU[k]
                    a_in, m_in, ip_in = TR[k][:, 0:n], A2[k][:, 0:n], TI[k][:, 0:n]
                else:
                    a_out, r_a = AAs[:, :, 1:9], r_AAs
                    u_out, r_u = UUs[:, :, 1:9], r_UUs
                    a_in, m_in, ip_in = v3(TR[k][:, 0:n]), v3(A2[k][:, 0:n]), v3(TI[k][:, 0:n])
                op("act", lambda e: e.activation(out=a_out, in_=a_in, func=AF.Exp, scale=dc(DC_CH + ct),
                                                 bias=dc(DC_CH + ct)), [r_TR[k], r_dcols], [r_a])
                op("dve", lambda e: e.tensor_tensor(out=m_in, in0=a_out, in1=a_out, op=ALU.mult), [r_a], [r_A2[k]])
                op("act", lambda e: e.activation(out=A2[k][:, 0:n], in_=A2[k][:, 0:n], func=AF.Ln, scale=-0.25,
                                                 bias=0.25), [r_A2[k]], [r_A2[k]])
                op("act", lambda e: e.activation(out=A2[k][:, 0:n], in_=A2[k][:, 0:n], func=AF.Exp, scale=0.5),
                   [r_A2[k]], [r_A2[k]])
                op("dve", lambda e: e.tensor_tensor(out=u_out, in0=m_in, in1=ip_in, op=ALU.mult),
                   [r_A2[k], r_TI[k]], [r_u])
                if kind == "p":
                    op("dve", lambda e: e.tensor_tensor_scan(out=HH[k][:, 0:n], data0=AA[k][:, 0:n], data1=UU[k][:, 0:n],
                                                             initial=hst[:, ct:ct + 1], op0=ALU.mult, op1=ALU.add),
                       [r_AA[k], r_UU[k], r_hst[ct]], [r_HH[k]])
                    op("pool", lambda e: e.tensor_copy(out=hst[:, ct:ct + 1], in_=HH[k][:, n - 1:n]),
                       [r_HH[k]], [r_hst[ct]])
                    op("dve", lambda e: e.scalar_tensor_tensor(out=yTL[bi % 2][:, ct, 0:n], in0=HH[k][:, 0:n], scalar=0.5,
                                                               in1=SZ[k][:, 0:n], op0=ALU.mult, op1=ALU.mult),
                       [r_HH[k], r_SZ[k]], [r_yTL[bi % 2][ct]])
                    if last_p[0]:
                        op("pool", lambda e: e.tensor_copy(out=hfin[:, ct, 0:1], in_=HH[k][:, n - 1:n]),
                           [r_HH[k]], [r_fin])
                else:
                    op("pool", lambda e: e.tensor_copy(out=UUs[:, :, 0], in_=shh[:, ct, :]), [r_const], [r_UUs])
                    flat = lambda t: t[:].rearrange("p a b -> p (a b)")
                    op("dve", lambda e: e.tensor_tensor_scan(out=flat(HHs), data0=flat(AAs), data1=flat(UUs),
                                                             initial=0.0, op0=ALU.mult, op1=ALU.add),
                       [r_AAs, r_UUs], [r_HHs])
                    op("dve", lambda e: e.scalar_tensor_tensor(
                        out=v3(yTL[bi % 2][:, ct, 0:n]), in0=HHs[:, :, 1:9], scalar=0.5,
                        in1=v3(SZ[k][:, 0:n]), op0=ALU.mult, op1=ALU.mult),
                       [r_HHs, r_SZ[k]], [r_yTL[bi % 2][ct]])
                    op("pool", lambda e: e.tensor_copy(out=hfin[:, ct, 1:17], in_=HHs[:, :, 8]), [r_HHs], [r_fin])

        s0_i = [0]

        def phase2(bi, kind):
            xTt, r_x = xT[bi % 2], r_xT[bi % 2]
            n = TB if kind == "p" else 128
            nch = n // 128
            PA2 = getattr(Sched, "PA2", True) and kind == "p"
            ring_ids[0] = [0, 1] if PA2 else [0, 1, 2]
            w, r_w = load_w([(5120, 16)])
            prg, r_prg = ring_next()
            proj(w, r_w, 0, 16, xTt, r_x, 0, n, prg[0:16, 0:n], r_prg)
            op("act", lambda e: e.activation(out=RGK[:, 0:n], in_=prg[0:16, 0:n], func=AF.Copy), [r_prg], [r_RGK])
            for h in range(4):
                wC, r_wC = load_w([(4096 + h * 256, 256)])
                for vh in range(2):
                    pzg, r_pzg = ring_next()
                    proj(wC, r_wC, vh * 128, 128, xTt, r_x, 0, n, pzg[:, 0:n], r_pzg)
                    tz, r_tz = TZ[vh], r_TZ[vh]
                    kk = 8 + 2 * h + vh
                    op("act", lambda e, pzg=pzg, tz=tz: e.activation(out=tz[:, 0:n], in_=pzg[:, 0:n], func=AF.Tanh, scale=0.5),
                       [r_pzg], [r_tz])
                    op("dve", lambda e, pzg=pzg, tz=tz, kk=kk: e.scalar_tensor_tensor(
                        out=yTG[:, kk - 8, 0:n], in0=tz[:, 0:n], scalar=1.0, in1=pzg[:, 0:n], op0=ALU.add, op1=ALU.mult),
                       [r_tz, r_pzg], [r_yTG[kk - 8]])
            for h in range(4):
                wA, r_wA = load_w([(2048 + h * 128, 128), (2560 + h * 128, 128)])
                wB, r_wB = load_w([(3072 + h * 256, 256)])
                if bi == 0:
                    load_wout(4)
                pg, r_pg = ring_next()
                op("pe", lambda e: e.matmul(pg[:, 0:n], lhsT=wgk2_bf[0:16, h * 128:(h + 1) * 128], rhs=RGK[0:16, 0:n],
                                            start=True, stop=True), [r_wgk2, r_RGK], [r_pg])
                EE, r_EE = EEb, r_EEb
                BB, r_BB = BBb, r_BBb
                E1, r_E1 = E123[0], r_E123[0]
                E2, r_E2 = E123[1], r_E123[1]
                E3, r_E3 = E123[2], r_E123[2]
                DD, r_DD = EEb, r_EEb
                op("act", lambda e: e.activation(out=EE[:, 0:n], in_=pg[:, 0:n], func=AF.Exp, scale=-1.0,
                                                 bias=dc(DC_NBGK + h)), [r_pg, r_dcols], [r_EE])
                op("act", lambda e: e.activation(out=EE[:, 0:n], in_=EE[:, 0:n], func=AF.Ln, bias=1.0), [r_EE], [r_EE])
                mcol = CS_CM if kind == "p" else CS_SM
                op("dve", lambda e: e.tensor_tensor_scan(out=BB[:, 0:n], data0=cst[:, mcol:mcol + n], data1=EE[:, 0:n],
                                                         initial=0.0, op0=ALU.mult, op1=ALU.add),
                   [r_EE, r_const], [r_BB])
                cl = 128 if kind == "p" else 8
                ncl = n // cl
                b3 = BB[:, 0:n].rearrange("p (a b) -> p a b", b=cl)
                op("pool", lambda e: e.tensor_tensor(out=DD[:, 0:n].rearrange("p (a b) -> p a b", b=cl), in0=b3,
                                                    in1=b3[:, :, cl - 1:cl].to_broadcast([128, ncl, cl]),
                                                    op=ALU.subtract), [r_BB], [r_DD])
                op("act", lambda e: e.activation(out=E1[:, 0:n], in_=BB[:, 0:n], func=AF.Exp, scale=-1.0 / 16),
                   [r_BB], [r_E1])
                op("act", lambda e: e.activation(out=E2[:, 0:n], in_=BB[:, 0:n], func=AF.Exp, scale=1.0 / 16),
                   [r_BB], [r_E2])
                op("act", lambda e: e.activation(out=E3[:, 0:n], in_=DD[:, 0:n], func=AF.Exp, scale=1.0 / 16),
                   [r_DD], [r_E3])
                op("act", lambda e: e.activation(out=DEC[:, 0:ncl], in_=b3[:, :, cl - 1], func=AF.Exp, scale=-1.0 / 16),
                   [r_BB], [r_DEC])
                pq, r_pq = ring_next()
                proj(wA, r_wA, 0, 128, xTt, r_x, 0, n, pq[:, 0:n], r_pq)
                op("dve", lambda e: e.scalar_tensor_tensor(out=QT[:, 0:n], in0=pq[:, 0:n], scalar=QSCALE, in1=E1[:, 0:n],
                                                           op0=ALU.mult, op1=ALU.mult), [r_pq, r_E1], [r_QT])
                pk, r_pk = ring_next()
                proj(wA, r_wA, 128, 128, xTt, r_x, 0, n, pk[:, 0:n], r_pk)
                op("dve", lambda e: e.tensor_tensor(out=KT[:, 0:n], in0=pk[:, 0:n], in1=E2[:, 0:n], op=ALU.mult),
                   [r_pk, r_E2], [r_KT])
                op("dve", lambda e: e.tensor_tensor(out=KE[:, 0:n], in0=pk[:, 0:n], in1=E3[:, 0:n], op=ALU.mult),
                   [r_pk, r_E3], [r_KE])
                for j in range(nch):
                    pv = pb_v[:, (j % 2) * 256:(j % 2 + 1) * 256]
                    for kc in range(8):
                        op("pe", lambda e, kc=kc, j=j, pv=pv: e.matmul(pv, lhsT=xTt[:, kc, j * 128:(j + 1) * 128],
                                                                      rhs=wB[:, kc, 0:256], start=(kc == 0), stop=(kc == 7)),
                           [r_wB, r_x], [r_pbv[j % 2]], signal=(kc == 7))
                    op("dve", lambda e, j=j, pv=pv: e.tensor_copy(out=VS[:, j, :], in_=pv),
                       [r_pbv[j % 2]], [r_VS[j]])
                for c in range(nch):
                    k = c % 2
                    pai = (4 if c % 2 == 0 else 2) if PA2 else 4
                    pba_c, r_pba_c = bank[pai], r_bank[pai]
                    cs = slice(c * 128, (c + 1) * 128)
                    op("pe", lambda e: e.transpose(out=ptr[:, 0:128], in_=KE[:, cs], identity=ident_bf[:]),
                       [r_KE, r_idb], [r_ptr])
                    op("dve", lambda e: e.tensor_copy(out=KEt[k][:], in_=ptr[:, 0:128]), [r_ptr], [r_KEt[k]])
                    op("pe", lambda e: e.matmul(pba_c[:, 0:128], lhsT=KT[:, cs], rhs=QT[:, cs], start=True, stop=True),
                       [r_KT, r_QT], [r_pba_c])
                    mk = CS_U if kind == "p" else CS_US
                    op("dve", lambda e: e.tensor_tensor(out=ATT[k][:], in0=pba_c[:, 0:128], in1=cst[:, mk:mk + 128],
                                                        op=ALU.mult), [r_pba_c, r_const], [r_ATT[k]])
                    if kind == "p":
                        pbi = 5 + (c % 2 if getattr(Sched, "PO2", True) else 0)
                        pbo = bank[pbi]
                        pos = (pbo[:, 0:128], pbo[:, 128:256])
                        r_pos = [r_bank[pbi], r_bank[pbi]]
                    else:
                        pos = (pb_o0[:, 0:128], pb_o1[:, 0:128])
                        r_pos = [r_bank[5], r_bank[6]]
                    if kind == "p":
                        for vh in range(2):
                            op("pe", lambda e, vh=vh: e.matmul(pos[vh], lhsT=VS[:, c, vh * 128:(vh + 1) * 128], rhs=ATT[k][:],
                                                               start=True, stop=False), [r_VS[c], r_ATT[k]], [r_pos[vh]],
                               signal=False)
                            op("pe", lambda e, vh=vh: e.matmul(pos[vh], lhsT=Sbf[:, h, vh * 128:(vh + 1) * 128], rhs=QT[:, cs],
                                                               start=False, stop=True), [r_Sbf[h], r_QT], [r_pos[vh]])
                        pu = pb_v[:, 0:256]
                        op("pe", lambda e: e.matmul(pu, lhsT=KEt[k][:], rhs=VS[:, c, :], start=True, stop=True),
                           [r_KEt[k], r_VS[c]], [r_pbv[0]])
                        op("dve", lambda e: e.scalar_tensor_tensor(out=Sst[:, h, :], in0=Sst[:, h, :], scalar=DEC[:, c:c + 1],
                                                                   in1=pu, op0=ALU.mult, op1=ALU.add),
                           [r_S[h], r_DEC, r_pbv[0]], [r_S[h]])
                        op("act", lambda e: e.activation(out=Sbf[:, h, :], in_=Sst[:, h, :], func=AF.Copy), [r_S[h]], [r_Sbf[h]])
                        if last_p[0] and c == nch - 1:
                            dma("pool", os_d[0, h], Sst[:, h, :], [r_S[h]], [], "os_p", is_output=True)
                    else:
                        qm_out = bass.AP(QM.tensor, QM.offset, [[QM.ap[0][0], 128], [136, NSEQ], [1, 8]])
                        op("dve", lambda e: e.tensor_copy(out=qm_out, in_=QT[:, 0:128].rearrange("p (a b) -> p a b", b=8)),
                           [r_QT], [r_QM])
                        kem3 = KEM.rearrange("p (a b) -> p a b", b=128)
                        op("dve", lambda e: e.tensor_tensor(
                            out=kem3, in0=KEt[k][:].unsqueeze(1).to_broadcast([128, NSEQ, 128]),
                            in1=cst[:, CS_M:CS_M + NSEQ].unsqueeze(2).to_broadcast([128, NSEQ, 128]), op=ALU.mult),
                           [r_KEt[k], r_const], [r_KEM])
                        for vh in range(2):
                            op("pe", lambda e, vh=vh: e.matmul(pos[vh], lhsT=VS[:, c, vh * 128:(vh + 1) * 128], rhs=ATT[k][:],
                                                               start=True, stop=False), [r_VS[c], r_ATT[k]], [r_pos[vh]],
                               signal=False)
                        for i in range(NSEQ):
                            s = s0_i[0] % NS0
                            s0_i[0] += 1
                            dma("sp", S0[s], sgla_d[i, h], [], [r_S0[s]], f"S0{s}")
                            sbi = s % NSB
                            op("act", lambda e, s=s, sbi=sbi: e.activation(out=S0b[sbi][:], in_=S0[s], func=AF.Copy), [r_S0[s]], [r_S0b[sbi]])
                            for vh in range(2):
                                op("pe", lambda e, vh=vh, s=s, i=i, sbi=sbi: e.matmul(
                                    pos[vh], lhsT=S0b[sbi][:, vh * 128:(vh + 1) * 128], rhs=QM[:, i * 128:(i + 1) * 128],
                                    start=False, stop=(i == NSEQ - 1)), [r_S0b[sbi], r_QM], [r_pos[vh]],
                                   signal=(i == NSEQ - 1))
                            pub = [0, 1, 2, 3][i % 4]
                            pu = bank[pub][:, 0:256]
                            op("pe", lambda e, i=i, pu=pu: e.matmul(pu, lhsT=KEM[:, i * 128:(i + 1) * 128], rhs=VS[:, c, :],
                                                                    start=True, stop=True), [r_KEM, r_VS[c]], [r_bank[pub]])
                            op("dve", lambda e, s=s, i=i, pu=pu: e.scalar_tensor_tensor(
                                out=S0[s], in0=S0[s], scalar=DEC[:, i:i + 1], in1=pu, op0=ALU.mult, op1=ALU.add),
                               [r_S0[s], r_DEC, r_bank[pub]], [r_S0[s]])
                            dma("sp", os_d[1 + i, h], S0[s], [r_S0[s]], [], f"SNo{s}", is_output=True)
                    if kind == "p":
                        op("act", lambda e: e.activation(out=SQ[k][:], in_=pbo[:, 0:256], func=AF.Square),
                           [r_pos[0]], [r_SQ[k]])
                    else:
                        for vh in range(2):
                            op("act", lambda e, vh=vh: e.activation(out=SQ[k][:, vh * 128:(vh + 1) * 128], in_=pos[vh],
                                                                    func=AF.Square), [r_pos[vh]], [r_SQ[k]])
                    op("pe", lambda e: e.matmul(pba_c[:, 128:256], lhsT=ones_bf[:], rhs=SQ[k][:, 0:128], start=True, stop=False),
                       [r_idb, r_SQ[k]], [r_pba_c], signal=False)
                    op("pe", lambda e: e.matmul(pba_c[:, 128:256], lhsT=ones_bf[:], rhs=SQ[k][:, 128:256], start=False, stop=True),
                       [r_idb, r_SQ[k]], [r_pba_c])
                    op("act", lambda e: e.activation(out=RT[k][:], in_=pba_c[:, 128:256], func=AF.Ln, scale=1.0 / HV, bias=EPS),
                       [r_pba_c], [r_RT[k]])
                    op("act", lambda e: e.activation(out=RS[k][:], in_=RT[k][:], func=AF.Exp, scale=-0.5), [r_RT[k]], [r_RS[k]])
                    for vh in range(2):
                        kk = 8 + 2 * h + vh
                        op("dve", lambda e, vh=vh: e.scalar_tensor_tensor(out=Y1[vh][:], in0=pos[vh], scalar=dc(DC_GH2 + vh),
                                                                         in1=RS[k][:], op0=ALU.mult, op1=ALU.mult),
                           [r_pos[vh], r_RS[k], r_dcols], [r_Y1[vh]])
                        op("dve", lambda e, vh=vh, kk=kk: e.tensor_tensor(out=yTG[:, kk - 8, cs], in0=Y1[vh][:], in1=yTG[:, kk - 8, cs],
                                                                        op=ALU.mult), [r_Y1[vh], r_yTG[kk - 8]], [r_yTG[kk - 8]])

        xr_i = [0]

        def phase3(bi, row0, ntile):
            ring_ids[0] = list(getattr(Sched, "RING3", [7, 5, 6]))
            for j in range(ntile):
                xs_ = xi_i[0] % NXI
                xi_i[0] += 1
                s = 0
                rows = slice(row0 + j * 128, row0 + (j + 1) * 128)
                dma("sp", xin[xs_][:], x_d[rows, :], [], [r_xin[xs_]], f"xin{xs_}")
                pouts = []
                for half in range(2):
                    po_, r_po_ = ring_next()
                    for kc in range(16):
                        ysrc = yTL[bi % 2][:, kc, j * 128:(j + 1) * 128] if kc < 8 else yTG[:, kc - 8, j * 128:(j + 1) * 128]
                        r_ys = r_yTL[bi % 2][kc] if kc < 8 else r_yTG[kc - 8]
                        op("pe", lambda e, kc=kc, half=half, po_=po_, ysrc=ysrc: e.matmul(
                            po_[:], lhsT=ysrc, rhs=wout_bf[:, kc, half * 512:(half + 1) * 512],
                            start=(kc == 0), stop=(kc == 15)), [r_ys, r_wout_k[kc]], [r_po_], signal=(kc == 15))
                    op("act", lambda e, half=half, po_=po_: e.activation(out=junk[:], in_=po_[:],
                                                                        func=AF.Square, accum_out=sm(20 + half)),
                       [r_po_], [r_junk, rsm(20 + half)])
                    pouts.append((po_, r_po_))
                op("dve", lambda e: e.tensor_tensor(out=sm(22), in0=sm(20), in1=sm(21), op=ALU.add),
                   [rsm(20), rsm(21)], [rsm(22)])
                op("act", lambda e: e.activation(out=sm(23), in_=sm(22), func=AF.Ln, scale=1.0 / D, bias=EPS),
                   [rsm(22)], [rsm(23)])
                op("act", lambda e: e.activation(out=sm(24), in_=sm(23), func=AF.Exp, scale=-0.5), [rsm(23)], [rsm(24)])
                for half in range(2):
                    po_, r_po_ = pouts[half]
                    hs = slice(half * 512, (half + 1) * 512)
                    op("dve", lambda e, po_=po_, hs=hs, s=s: e.scalar_tensor_tensor(
                        out=osb[s][:, hs], in0=po_[:], scalar=sm(24), in1=gpost[:, hs], op0=ALU.mult, op1=ALU.mult),
                       [r_po_, rsm(24), r_const], [r_osb[s]])
                op("dve", lambda e, s=s, xs_=xs_: e.tensor_tensor(out=osb[s][:], in0=osb[s][:], in1=xin[xs_][:], op=ALU.add),
                   [r_osb[s], r_xin[xs_]], [r_osb[s]])
                dma("sp", y_d[rows, :], osb[s][:], [r_osb[s]], [], f"osb{s}", is_output=True)

        pblocks = [("p", b * TB, TB // 128) for b in range(NPB)]
        spos = getattr(Sched, "SPOS", 4)
        assert spos <= 4
        blocks = pblocks[:spos] + [("s", LP, 1)] + pblocks[spos:]
        phase0(0, blocks[0][1], blocks[0][2])
        late_setup()
        for bi, (kind, row0, ntile) in enumerate(blocks):
            first_pass[0] = (bi == 0)
            last_p[0] = (kind == "p" and row0 == LP - TB)
            S.tag = f"b{bi}"
            if kind == "s":
                op("pool", lambda e: e.memset(S0b[0][:, 0:1], 0.0), [], r_ax + [r_QM, r_KEM, r_S0b[0]] + r_S0)
                op("pool", lambda e: e.memset(QM, 0.0), [], [r_QM])
            phase1(bi, kind)
            phase2(bi, kind)
            if bi + 1 < len(blocks):
                phase0(bi + 1, blocks[bi + 1][1], blocks[bi + 1][2])
            phase3(bi, row0, ntile)
        dma("pool", oh_d, hfin[:].rearrange("p a b -> p (a b)"), [r_fin], [], "fin", is_output=True)
        dma("pool", oc_d, cfin[:].rearrange("p a b c -> p (a b c)"), [r_fin], [], "fin", is_output=True)
        S.finish()
        tg = {}
        for nd in S.nodes:
            t = tg.setdefault((nd["tag"], nd["eng"]), [1e9, 0.0, 0.0])
            t[0] = min(t[0], nd["start"]); t[1] = max(t[1], nd["fin"]); t[2] += nd["cost"]
        build_nc.stats = dict(nodes=len(S.nodes), nwait=S.nwait, est_us=S.est_total, tags=tg)
    return nc


_NC_CACHE = {}


def _consts():
    c = np.zeros((128, CS_N), np.float32)
    c[:, CS_ID:CS_ID + 128] = np.eye(128, dtype=np.float32)
    s = np.arange(128)[:, None]
    t = np.arange(128)[None, :]
    c[:, CS_U:CS_U + 128] = (s <= t)
    c[:, CS_US:CS_US + 128] = (s <= t) & (s // SL == t // SL)
    c[:, CS_M:CS_M + NSEQ] = (s // SL == np.arange(NSEQ)[None, :])
    cm = np.ones(TB, np.float32)
    cm[::128] = 0.0
    c[:, CS_CM:CS_CM + TB] = cm[None, :]
    smk = np.ones(128, np.float32)
    smk[::SL] = 0.0
    c[:, CS_SM:CS_SM + 128] = smk[None, :]
    return c


def make_in_maps(x_prompt, x_sample, state_lru_h, state_lru_conv, state_gla, g_pre, w_in, conv_w, conv_b,
           w_rg, b_rg, w_ig, b_ig, lru_lambda, w_gk2, b_gk, g_head, w_out, g_post):
    f = lambda a: np.ascontiguousarray(np.asarray(a, dtype=np.float32))
    x_prompt, x_sample = f(x_prompt), f(x_sample)
    col = lambda v, n: f(v).reshape(n, 128).T
    pcols = np.concatenate([
        col(g_pre[0], 8),
        np.concatenate([col(conv_w[0][j], 8) for j in range(4)], axis=1),
        col(conv_b[0], 8), col(b_rg[0], 8), col(b_ig[0], 8), col(lru_lambda[0], 8),
        col(b_gk[0], 4), col(g_head[0], 2)], axis=1)
    pcols = np.ascontiguousarray(pcols, dtype=np.float32)
    assert pcols.shape == (128, PC_N)
    wg = np.zeros((128, 2, 8, 128), np.float32)
    for gi, wsrc in enumerate((f(w_rg)[0], f(w_ig)[0])):
        for ct in range(8):
            wg[0:64, gi, ct, 0:64] = wsrc[2 * ct]
            wg[64:128, gi, ct, 64:128] = wsrc[2 * ct + 1]
    wg = wg.reshape(128, 2 * 8 * 128)
    gpost = np.ascontiguousarray(np.broadcast_to(f(g_post)[0][None, :], (128, D)))
    cst = _consts()
    win = f(w_in)[0]
    wout = f(w_out)[0]
    wgk2 = f(w_gk2)[0]
    in_maps = []
    for c in range(NCORES):
        sl = slice(c * NSEQ, (c + 1) * NSEQ)
        xs = np.concatenate([x_prompt[c], x_sample[sl].reshape(NSEQ * SL, D)], axis=0)
        shc = f(state_lru_conv)[0, sl]
        shc = shc.reshape(NSEQ, 3, 8, 128).transpose(3, 2, 0, 1).reshape(128, 8 * NSEQ * 3)
        shh = f(state_lru_h)[0, sl].reshape(NSEQ, 8, 128).transpose(2, 1, 0).reshape(128, 8 * NSEQ)
        in_maps.append({
            "x": np.ascontiguousarray(xs), "w_in": win, "w_out": wout, "wg": wg, "wgk2": wgk2, "pcols": pcols,
            "cst": cst, "gpost": gpost, "shc": np.ascontiguousarray(shc), "shh": np.ascontiguousarray(shh),
            "sgla": np.ascontiguousarray(f(state_gla)[0, sl]),
        })
    return in_maps


def kernel(**inputs):
    in_maps = make_in_maps(**inputs)
    if "nc" not in _NC_CACHE:
        _NC_CACHE["nc"] = build_nc()
    res = run_bass_kernel_spmd(_NC_CACHE["nc"], in_maps, core_ids=list(range(NCORES)))
    return assemble(res.results)


def assemble(R):
    y_p = np.stack([R[c]["y"][:LP] for c in range(NCORES)], 0)
    y_s = np.concatenate([R[c]["y"][LP:].reshape(NSEQ, SL, D) for c in range(NCORES)], 0)
    oh = [R[c]["o_h"].reshape(128, 8, 17).transpose(2, 1, 0).reshape(17, D) for c in range(NCORES)]
    oc = [R[c]["o_c"].reshape(128, 8, 17, 3).transpose(2, 3, 1, 0).reshape(17, 3, D) for c in range(NCORES)]
    h_p = np.stack([o[0] for o in oh], 0)[None]
    h_s = np.concatenate([o[1:] for o in oh], 0)[None]
    c_p = np.stack([o[0] for o in oc], 0)[None]
    c_s = np.concatenate([o[1:] for o in oc], 0)[None]
    s_p = np.stack([R[c]["o_s"][0] for c in range(NCORES)], 0)[None]
    s_s = np.concatenate([R[c]["o_s"][1:] for c in range(NCORES)], 0)[None]
    out = (y_p, y_s, h_p, c_p, s_p, h_s, c_s, s_s)
    return tuple(np.ascontiguousarray(o, dtype=np.float32) for o in out)
```

```python
import numpy as np
from contextlib import ExitStack
import concourse.bass as bass
import concourse.mybir as mybir
from concourse.bass_utils import run_bass_kernel_spmd

F32, BF16 = mybir.dt.float32, mybir.dt.bfloat16
ALU = mybir.AluOpType
AF = mybir.ActivationFunctionType

NCORES = 8
D = 1024
DIN = 5136
LP = 2048
NSEQ = 16
SL = 8
TB = 512
NPB = LP // TB
NTOK = LP + NSEQ * SL
EPS = 1e-6
HK = 128
HV = 256
QSCALE = float(HK) ** -0.5

PC_GPRE, PC_CW, PC_CB, PC_BRG, PC_BIG, PC_LAM, PC_BGK, PC_GH, PC_N = 0, 8, 40, 48, 56, 64, 72, 76, 78
DC_HBRG, DC_HBIG, DC_CH, DC_CF, DC_NBGK, DC_GH2, DC_N = 0, 8, 16, 24, 32, 36, 40
CS_ID, CS_U, CS_US, CS_M, CS_CM, CS_SM, CS_N = 0, 128, 256, 384, 400, 912, 1040


class Res:
    __slots__ = ("name", "w", "r")

    def __init__(self, name):
        self.name = name
        self.w = None
        self.r = []


class _Rec:
    def __init__(self):
        self.calls = []

    def __getattr__(self, name):
        def f(*args, **kwargs):
            self.calls.append((name, args, kwargs))
            return self
        return f


def _ap_elems(ap):
    n = 1
    for d in ap.shape[1:]:
        n *= d
    return n


class Sched:
    ENGS = ("pe", "act", "dve", "pool", "sp")

    def __init__(self, nc, es):
        self.nc = nc
        self.es = es
        self.E = {"pe": nc.tensor, "act": nc.scalar, "dve": nc.vector, "pool": nc.gpsimd, "sp": nc.sync}
        self.nodes = []
        self.pend_pe = None
        self.out_dma_nodes = []
        self.tag = ""

    def _collect(self, reads, writes):
        deps = {}
        for r in reads:
            if r.w is not None:
                deps[r.w] = True
        for w in writes:
            if w.w is not None:
                deps.setdefault(w.w, False)
            for t in w.r:
                deps.setdefault(t, False)
        return deps

    def _cost(self, eng, name, kwargs):
        try:
            if eng == "pe":
                if name == "transpose":
                    return 0.11
                n = _ap_elems(kwargs["rhs"])
                return 0.06 + n * 0.00039
            out = kwargs.get("out", None)
            if out is None:
                out = kwargs.get("ap", None)
            n = _ap_elems(out) if out is not None else 512
            if eng == "act":
                return 0.22 + n * 0.00085
            if eng == "dve":
                if name == "tensor_tensor_scan":
                    return 0.25 + n * 0.0021
                return 0.15 + n * 0.00105
            if eng == "pool":
                return 0.2 + n * 0.0032
        except Exception:
            pass
        return 0.5

    def op(self, eng, fn, reads=(), writes=(), signal=True):
        rec = _Rec()
        fn(rec)
        assert len(rec.calls) == 1
        name, args, kwargs = rec.calls[0]
        deps = self._collect(reads, writes)
        cost = self._cost(eng, name, kwargs)
        if eng == "pe" and self.pend_pe is not None:
            node = self.pend_pe
            node["calls"].append((name, args, kwargs))
            for d, raw in deps.items():
                if d != node["id"]:
                    node["deps"][d] = node["deps"].get(d, False) or raw
            node["cost"] += cost
        else:
            node = {"id": len(self.nodes), "eng": eng, "calls": [(name, args, kwargs)], "deps": dict(deps),
                    "cost": cost, "dma": False, "tset": None, "tag": self.tag}
            if eng == "act" and name == "activation":
                f = kwargs.get("func")
                node["tset"] = {AF.Tanh: "T", AF.Ln: "L", AF.Sqrt: "S"}.get(f, None)
            self.nodes.append(node)
        nid = node["id"]
        if not signal:
            assert eng == "pe"
            self.pend_pe = node
        else:
            self.pend_pe = None
        for r in reads:
            if nid not in r.r:
                r.r.append(nid)
        for w in writes:
            w.w = nid
            w.r = []
        return nid

    def dma(self, q, out, in_, reads, writes, semkey, is_output=False):
        assert self.pend_pe is None or q != "pe"
        deps = self._collect(reads, writes)
        nbytes = 128 * _ap_elems(out) * (2 if out.dtype == BF16 else 4)
        node = {"id": len(self.nodes), "eng": q, "calls": [("dma_start", (), {"out": out, "in_": in_})],
                "deps": dict(deps), "cost": 0.12 if q == "sp" else 1.0, "dma": True, "semkey": semkey, "bytes": nbytes,
                "tag": self.tag}
        self.nodes.append(node)
        nid = node["id"]
        if is_output:
            self.out_dma_nodes.append(nid)
        for r in reads:
            if nid not in r.r:
                r.r.append(nid)
        for w in writes:
            w.w = nid
            w.r = []
        return nid

    def _schedule(self):
        nodes = self.nodes
        n = len(nodes)
        nsucc = [[] for _ in range(n)]
        indeg = [0] * n
        for nd in nodes:
            for d in nd["deps"]:
                nsucc[d].append(nd["id"])
                indeg[nd["id"]] += 1
        last_sem = {}
        for nd in nodes:
            if nd["dma"]:
                k = nd["semkey"]
                if k in last_sem and last_sem[k] not in nd["deps"]:
                    nd["deps"][last_sem[k]] = False
                    nsucc[last_sem[k]].append(nd["id"])
                    indeg[nd["id"]] += 1
                last_sem[k] = nd["id"]
        bl = [0.0] * n
        for nd in reversed(nodes):
            i = nd["id"]
            c = (2.0 + nd["bytes"] / 240e3) if nd["dma"] else nd["cost"] + 0.15
            m = 0.0
            for j in nsucc[i]:
                if bl[j] > m:
                    m = bl[j]
            bl[i] = c + m
        self.bl = bl
        PRI_WIN = getattr(Sched, "PRI_WIN", 0.3)
        finish = [0.0] * n
        ready_t = [0.0] * n
        free_at = {e: 0.0 for e in self.ENGS}
        pipe_t = 0.0
        import heapq
        ready = {e: [] for e in self.ENGS}
        for nd in nodes:
            if indeg[nd["id"]] == 0:
                heapq.heappush(ready[nd["eng"]], (0.0, nd["id"]))
        order = []
        done = 0
        cur_set = [None]
        SWITCH = getattr(Sched, "SWITCH", 1.4)
        ACT_WIN = getattr(Sched, "ACT_WIN", 2.5)
        while done < n:
            best = None
            for e in self.ENGS:
                h = ready[e]
                if not h:
                    continue
                base = max(free_at[e], h[0][0])
                lim = base + (ACT_WIN if e == "act" else PRI_WIN + 1e-9)
                cand = None
                for (rt, i) in h:
                    if rt > lim:
                        continue
                    pen = 0.0
                    if e == "act":
                        ts_ = nodes[i]["tset"]
                        if ts_ is not None and cur_set[0] is not None and ts_ != cur_set[0]:
                            pen = SWITCH
                    stt = max(free_at[e], rt) + pen
                    if PRI_WIN > 0:
                        key = (round(stt / PRI_WIN) if e != "act" else stt, -bl[i], i)
                    else:
                        key = (stt, i)
                    if cand is None or key < cand[0]:
                        cand = (key, rt, i, pen, stt)
                st = cand[4] if PRI_WIN > 0 else cand[0][0]
                if best is None or st < best[0] or (st == best[0] and cand[2] < best[2]):
                    best = (st, e, cand[2], cand[1])
            st, e, i, rt = best
            ready[e].remove((rt, i))
            heapq.heapify(ready[e])
            nd = nodes[i]
            if e == "act" and nd["tset"] is not None:
                cur_set[0] = nd["tset"]
            if nd["dma"]:
                free_at[e] = st + nd["cost"]
                pipe_t = max(pipe_t, st) + nd["bytes"] / getattr(Sched, "DMA_BPUS", 240e3)
                finish[i] = max(st + 2.0, pipe_t + 1.5)
            else:
                free_at[e] = st + nd["cost"]
                finish[i] = free_at[e] + 0.15
            nd["start"] = st
            nd["fin"] = finish[i]
            order.append(i)
            done += 1
            for j in nsucc[i]:
                indeg[j] -= 1
                if finish[i] > ready_t[j]:
                    ready_t[j] = finish[i]
                if indeg[j] == 0:
                    heapq.heappush(ready[nodes[j]["eng"]], (ready_t[j], j))
        self.est_total = max(finish) if finish else 0.0
        return order

    def finish(self):
        assert self.pend_pe is None
        order = self._schedule()
        nc, es = self.nc, self.es
        sems, cnt, hist, dcnt = {}, {}, {}, {}
        seen = {k: {} for k in self.ENGS}
        for k in ("pe", "act", "dve", "pool"):
            sems[k] = es.enter_context(nc.semaphore("s_" + k))
            cnt[k] = 0
            hist[k] = [{}]
        tok = {}
        self.nwait = 0

        def wait(eng, t):
            key, val = t
            if seen[eng].get(key, 0) >= val:
                return
            self.E[eng].wait_ge(sems[key], val)
            self.nwait += 1
            seen[eng][key] = val
            if key in hist and val < len(hist[key]):
                for k2, v2 in hist[key][val].items():
                    if seen[eng].get(k2, 0) < v2:
                        seen[eng][k2] = v2

        EMBED = getattr(Sched, "EMBED", True)

        def mark_seen(eng, t):
            key, val = t
            seen[eng][key] = val
            if key in hist and val < len(hist[key]):
                for k2, v2 in hist[key][val].items():
                    if seen[eng].get(k2, 0) < v2:
                        seen[eng][k2] = v2

        for i in order:
            nd = self.nodes[i]
            eng = nd["eng"]
            need = {}
            for d, raw in nd["deps"].items():
                t = tok[d]
                if need.get(t[0], 0) < t[1]:
                    need[t[0]] = t[1]
            items = sorted(need.items(), key=lambda kv: (kv[0] == eng, kv[0] not in hist, kv[0]))
            snap = dict(seen[eng])
            pend = []
            for k, v in items:
                if seen[eng].get(k, 0) >= v:
                    continue
                pend.append((k, v))
                mark_seen(eng, (k, v))
            seen[eng] = snap
            emb = None
            multi = any(("accum_out" in c[2] and c[2]["accum_out"] is not None) for c in nd["calls"][:1])
            if EMBED and pend and not nd["dma"] and not multi:
                emb = pend.pop()
            for k, v in pend:
                wait(eng, (k, v))
            inst = None
            for ci, (name, args, kwargs) in enumerate(nd["calls"]):
                inst = getattr(self.E[eng], name)(*args, **kwargs)
                if ci == 0 and emb is not None:
                    inst._wait_ge(sems[emb[0]], emb[1])
                    mark_seen(eng, emb)
            if nd["dma"]:
                key = nd["semkey"]
                if key not in sems:
                    sems[key] = es.enter_context(nc.semaphore("d_" + key))
                    dcnt[key] = 0
                inst.then_inc(sems[key], 16)
                dcnt[key] += 16
                tok[i] = (key, dcnt[key])
            else:
                inst.then_inc(sems[eng], 1)
                cnt[eng] += 1
                hist[eng].append(dict(seen[eng]))
                tok[i] = (eng, cnt[eng])
        for i in self.out_dma_nodes:
            wait("sp", tok[i])
        for k in ("pe", "act", "dve", "pool"):
            wait("sp", (k, cnt[k]))


def build_nc():
    nc = bass.Bass("TRN2", target_bir_lowering=False)
    dt_in = lambda name, shape: nc.dram_tensor(name, list(shape), F32, kind="ExternalInput").ap()
    dt_out = lambda name, shape: nc.dram_tensor(name, list(shape), F32, kind="ExternalOutput").ap()
    x_d = dt_in("x", [NTOK, D])
    win_d = dt_in("w_in", [D, DIN])
    wout_d = dt_in("w_out", [2 * D, D])
    wg_d = dt_in("wg", [128, 2 * 8 * 128])
    wgk2_d = dt_in("wgk2", [16, 512])
    pcols_d = dt_in("pcols", [128, PC_N])
    cst_d = dt_in("cst", [128, CS_N])
    gpost_d = dt_in("gpost", [128, D])
    shc_d = dt_in("shc", [128, 8 * NSEQ * 3])
    shh_d = dt_in("shh", [128, 8 * NSEQ])
    sgla_d = dt_in("sgla", [NSEQ, 4, 128, 256])

    y_d = dt_out("y", [NTOK, D])
    oh_d = dt_out("o_h", [128, 8 * 17])
    oc_d = dt_out("o_c", [128, 8 * 17 * 3])
    os_d = dt_out("o_s", [17, 4, 128, 256])
    wbf_d = nc.dram_tensor("wbf_scratch", [128, 41, 8, 128], BF16).ap()
    win_v = win_d.rearrange("(kc p) n -> p kc n", p=128)
    wout_v = wout_d.rearrange("(kc p) n -> p kc n", p=128)

    with ExitStack() as es:
        S = Sched(nc, es)
        op, dma = S.op, S.dma

        def sb(name, shape, dtype):
            return es.enter_context(nc.sbuf_tensor("sb_" + name, list(shape), dtype))

        def ps(name, shape, dtype):
            return es.enter_context(nc.psum_tensor("ps_" + name, list(shape), dtype))

        xT = [sb(f"xT{i}", [128, 8, TB], BF16) for i in range(2)]
        r_xT = [Res(f"xT{i}") for i in range(2)]
        yTL = [sb(f"yTL{i}", [128, 8, TB], BF16) for i in range(2)]
        r_yTL = [[Res(f"yTL{i}_{k}") for k in range(8)] for i in range(2)]
        yTG = sb("yTG", [128, 8, TB], BF16)
        r_yTG = [Res(f"yTG{k}") for k in range(8)]
        wout_bf = sb("wout_bf", [128, 16, D], BF16)
        r_wout_k = [Res(f"wout{g}") for g in range(16)]
        NWL = getattr(Sched, "NWL", 4)
        wl = [sb(f"wl{i}", [128, 8, 256], BF16) for i in range(NWL)]
        r_wl = [Res(f"wl{i}") for i in range(NWL)]
        NXI = 2
        xin = [sb(f"xin{i}", [128, D], F32) for i in range(NXI)]
        r_xin = [Res(f"xin{i}") for i in range(NXI)]
        osb = [sb(f"osb{i}", [128, D], F32) for i in range(1)]
        r_osb = [Res(f"osb{i}") for i in range(1)]
        xsb = [sb(f"xsb{i}", [128, D], BF16) for i in range(2)]
        r_xsb = [Res(f"xsb{i}") for i in range(2)]
        junk = sb("junk", [128, 512], BF16)
        r_junk = Res("junk")
        small = sb("small", [128, 64], F32)
        pcols = sb("pcols", [128, PC_N], F32)
        dcols = sb("dcols", [128, DC_N], F32)
        cst = sb("cst", [128, CS_N], F32)
        gpost = sb("gpost", [128, D], F32)
        wg_bf = sb("wg_bf", [128, 2 * 8 * 128], BF16)
        wgk2_bf = sb("wgk2_bf", [16, 512], BF16)
        ident_bf = sb("ident_bf", [128, 128], BF16)
        ones_bf = sb("ones_bf", [128, 128], BF16)
        shc = sb("shc", [128, 8, NSEQ, 3], F32)
        shh = sb("shh", [128, 8, NSEQ], F32)
        hst = sb("hst", [128, 8], F32)
        cst3 = sb("cst3", [128, 8, 3], F32)
        hfin = sb("hfin", [128, 8, 17], F32)
        cfin = sb("cfin", [128, 8, 17, 3], F32)
        Sst = sb("Sst", [128, 4, HV], F32)
        Sbf = sb("Sbf", [128, 4, HV], BF16)
        r_const = Res("const")
        r_dcols = Res("dcols")
        r_wgbf = Res("wgbf")
        r_wgk2 = Res("wgk2bf")
        r_idb = Res("identbf")
        r_hst = [Res(f"hst{c}") for c in range(8)]
        r_cst3 = [Res(f"cst3{c}") for c in range(8)]
        r_fin = Res("fin")
        r_S = [Res(f"S{h}") for h in range(4)]
        r_Sbf = [Res(f"Sbf{h}") for h in range(4)]

        def tmpf(name, n=TB, k=2, dtype=F32):
            return [sb(f"{name}{i}", [128, n], dtype) for i in range(k)], [Res(f"{name}{i}") for i in range(k)]

        XL, r_XL = tmpf("XL", TB + 4, 2, F32)
        TR, r_TR = tmpf("TR")
        TI, r_TI = tmpf("TI")
        A2, r_A2 = tmpf("A2")
        AA, r_AA = tmpf("AA")
        UU, r_UU = tmpf("UU")
        HH, r_HH = tmpf("HH")
        TZ, r_TZ = tmpf("TZ", TB, 2, BF16)
        XCB, r_XCB = tmpf("XCB", TB, 2, BF16)
        SZ, r_SZ = tmpf("SZ", TB, 2, BF16)
        EEb = sb("EEb", [128, TB], F32)
        BBb = sb("BBb", [128, TB], F32)
        E123 = [sb(f"E{i}b", [128, TB], BF16) for i in range(3)]
        r_EEb, r_BBb, r_E123 = Res("EEb"), Res("BBb"), [Res(f"E{i}b") for i in range(3)]
        XC, r_XC = tmpf("XC")
        XLs = sb("XLs", [128, NSEQ, 11], F32)
        AAs = sb("AAs", [128, NSEQ, 9], F32)
        UUs = sb("UUs", [128, NSEQ, 9], F32)
        HHs = sb("HHs", [128, NSEQ, 9], F32)
        r_XLs, r_AAs, r_UUs, r_HHs = Res("XLs"), Res("AAs"), Res("UUs"), Res("HHs")
        RGK = sb("RGK", [16, TB], BF16)
        r_RGK = Res("RGK")
        QT = sb("QT", [128, TB], BF16)
        KT = sb("KT", [128, TB], BF16)
        KE = sb("KE", [128, TB], BF16)
        r_QT, r_KT, r_KE = Res("QT"), Res("KT"), Res("KE")
        VS = sb("VS", [128, 4, HV], BF16)
        r_VS = [Res(f"VS{j}") for j in range(4)]
        DEC = sb("DEC", [128, 16], F32)
        r_DEC = Res("DEC")
        KEt = [sb(f"KEt{i}", [128, 128], BF16) for i in range(2)]
        r_KEt = [Res(f"KEt{i}") for i in range(2)]
        ATT = [sb(f"ATT{i}", [128, 128], BF16) for i in range(2)]
        r_ATT = [Res(f"ATT{i}") for i in range(2)]
        SQ = [sb(f"SQ{i}", [128, 256], BF16) for i in range(2)]
        r_SQ = [Res(f"SQ{i}") for i in range(2)]
        RT = [sb(f"RT{i}", [128, 128], F32) for i in range(2)]
        r_RT = [Res(f"RT{i}") for i in range(2)]
        RS = [sb(f"RS{i}", [128, 128], F32) for i in range(2)]
        r_RS = [Res(f"RS{i}") for i in range(2)]
        Y1 = [sb(f"Y1{i}", [128, 128], F32) for i in range(2)]
        r_Y1 = [Res(f"Y1{i}") for i in range(2)]
        arena = sb("arena", [128, 4096], F32)
        r_ax = [Res(f"ax{i}") for i in range(4)]
        QM = arena[:, 0:1024].bitcast(BF16)
        KEM = arena[:, 1024:2048].bitcast(BF16)
        r_QM, r_KEM = Res("QM"), Res("KEM")
        NS0 = 8
        S0 = [arena[:, 2048 + i * 256:2048 + (i + 1) * 256] for i in range(NS0)]
        r_S0 = [Res(f"S0{i}") for i in range(NS0)]
        NSB = 4
        S0b = [sb(f"S0b{i}", [128, HV], BF16) for i in range(NSB)]
        r_S0b = [Res(f"S0b{i}") for i in range(NSB)]

        bank = [ps(f"pb{i}", [128, 512], F32) for i in range(7)]
        r_bank = [Res(f"pb{i}") for i in range(7)]
        ptr = ps("ptr", [128, 1024], BF16)
        r_ptr = Res("ptr")
        bank.append(ptr[:].bitcast(F32))
        r_bank.append(r_ptr)
        ring_ids = [list(range(7))]
        ring_i = [0]

        def ring_next():
            ids = ring_ids[0]
            i = ids[ring_i[0] % len(ids)]
            ring_i[0] += 1
            return bank[i], r_bank[i]

        pb_v, pb_a, pb_o0, pb_o1 = bank[3], bank[4], bank[5], bank[6]
        r_pbv = [r_bank[3], r_bank[3]]
        r_patt = r_bank[4]
        r_pss = r_bank[4]
        r_po = [r_bank[5], r_bank[6]]

        pc = lambda c: pcols[:, c:c + 1]
        dc = lambda c: dcols[:, c:c + 1]
        sm = lambda c, n=1: small[:, c:c + n]
        r_sm = {}

        def rsm(c):
            if c not in r_sm:
                r_sm[c] = Res(f"small{c}")
            return r_sm[c]

        r_cparts = []
        for (dst, src) in ((pcols[:], pcols_d), (cst[:], cst_d), (gpost[:], gpost_d),                            (shc[:].rearrange("p a b c -> p (a b c)"), shc_d), (shh[:].rearrange("p a b -> p (a b)"), shh_d)):
            rc_ = Res("c_" + str(len(r_cparts)))
            r_cparts.append(rc_)
            dma("sp", dst, src, [], [rc_], f"const{len(r_cparts)}")
        op("pool", lambda e: e.memset(small[:, 63:64], 0.0), r_cparts, [r_const])
        dma("pool", ident_bf[:], cst_d[:, CS_ID:CS_ID + 128], [], [r_idb], "identq")

        def late_setup():
          dma("pool", wg_bf[:], wg_d, [], [r_wgbf], "wgbf")
          dma("pool", wgk2_bf[:], wgk2_d, [], [r_wgk2], "wgk2")
          op("dve", lambda e: e.memset(ones_bf[:], 1.0), [], [r_idb])
          op("dve", lambda e: e.memset(hst[:], 0.0), [], r_hst)
          op("dve", lambda e: e.memset(cst3[:].rearrange("p a b -> p (a b)"), 0.0), [], r_cst3)
          op("dve", lambda e: e.memset(Sst[:].rearrange("p a b -> p (a b)"), 0.0), [], r_S)
          op("dve", lambda e: e.memset(Sbf[:].rearrange("p a b -> p (a b)"), 0.0), [], r_Sbf)
          op("dve", lambda e: e.memset(AAs[:].rearrange("p a b -> p (a b)"), 0.0), [], [r_AAs])
          op("dve", lambda e: e.memset(UUs[:].rearrange("p a b -> p (a b)"), 0.0), [], [r_UUs])
          op("dve", lambda e: e.tensor_scalar(out=dcols[:, DC_HBRG:DC_HBRG + 16], in0=pcols[:, PC_BRG:PC_BRG + 16],
                                              scalar1=0.5, scalar2=None, op0=ALU.mult), [r_const], [r_dcols])
          op("dve", lambda e: e.tensor_scalar(out=dcols[:, DC_NBGK:DC_NBGK + 4], in0=pcols[:, PC_BGK:PC_BGK + 4],
                                              scalar1=-1.0, scalar2=None, op0=ALU.mult), [r_const], [r_dcols])
          op("dve", lambda e: e.tensor_scalar(out=dcols[:, DC_GH2:DC_GH2 + 2], in0=pcols[:, PC_GH:PC_GH + 2],
                                              scalar1=0.5, scalar2=None, op0=ALU.mult), [r_const], [r_dcols])
          op("act", lambda e: e.activation(out=sm(0, 8), in_=pcols[:, PC_LAM:PC_LAM + 8], func=AF.Exp, scale=-1.0),
             [r_const], [rsm(0)])
          op("act", lambda e: e.activation(out=sm(8, 8), in_=sm(0, 8), func=AF.Ln, bias=1.0), [rsm(0)], [rsm(8)])
          op("dve", lambda e: e.tensor_scalar(out=dcols[:, DC_CH:DC_CH + 8], in0=sm(8, 8), scalar1=-4.0, scalar2=None,
                                              op0=ALU.mult), [rsm(8)], [r_dcols])
          op("dve", lambda e: e.tensor_scalar(out=dcols[:, DC_CF:DC_CF + 8], in0=sm(8, 8), scalar1=-8.0, scalar2=None,
                                              op0=ALU.mult), [rsm(8)], [r_dcols])

        r_wd = {}
        gpre_bc = pcols[:, PC_GPRE:PC_GPRE + 8].unsqueeze(2).to_broadcast([128, 8, 128])
        wl_i = [0]
        last_p = [False]
        first_pass = [True]

        def load_w(parts):
            s = wl_i[0] % NWL
            wl_i[0] += 1
            off = 0
            for (c0, n) in parts:
                for tt in range((n + 127) // 128):
                    t = c0 // 128 + tt
                    nn = min(128, n - tt * 128)
                    if first_pass[0]:
                        cc = c0 + tt * 128
                        dma("pool", wl[s][:, :, off:off + nn], win_v[:, :, cc:cc + nn], [], [r_wl[s]], f"wlq{s}")
                        r_wd[t] = Res(f"wd{t}")
                        dma("sp", wbf_d[:, t, :, 0:nn], wl[s][:, :, off:off + nn], [r_wl[s]], [r_wd[t]], f"wds{s}")
                    else:
                        dma("sp", wl[s][:, :, off:off + nn], wbf_d[:, t, :, 0:nn], [r_wd[t]], [r_wl[s]], f"wl{s}")
                    off += nn
            return wl[s], r_wl[s]

        wo_i = [0]

        def load_wout(k):
            for _ in range(k):
                g = wo_i[0]
                if g >= 16:
                    return
                wo_i[0] += 1
                dma("pool", wout_bf[:, g, :], wout_v[:, g, :], [], [r_wout_k[g]], f"wout{g % 4}")

        def proj(w, r_w, woff, m, xTt, r_x, c0, n, out, r_out):
            for kc in range(8):
                op("pe", lambda e, kc=kc: e.matmul(out, lhsT=w[:, kc, woff:woff + m], rhs=xTt[:, kc, c0:c0 + n],
                                                    start=(kc == 0), stop=(kc == 7)),
                   [r_w, r_x], [r_out], signal=(kc == 7))

        xi_i = [0]

        def phase0(bi, row0, ntile):
            xTt, r_x = xT[bi % 2], r_xT[bi % 2]
            for j in range(ntile):
                s = xi_i[0] % NXI
                xi_i[0] += 1
                s2 = s % 2
                rows = slice(row0 + j * 128, row0 + (j + 1) * 128)
                if bi == 0:
                    xsrc, r_xs = arena[:, j * 1024:(j + 1) * 1024], r_ax[j]
                    dma("sp", xsrc, x_d[rows, :], [], [r_xs], f"ax{j}")
                else:
                    xsrc, r_xs = xin[s][:], r_xin[s]
                    dma("sp", xsrc, x_d[rows, :], [], [r_xs], f"xin{s}")
                op("dve", lambda e, xsrc=xsrc, s2=s2: e.scalar_tensor_tensor(out=xsb[s2][:], in0=xsrc, scalar=1.0, in1=xsrc,
                                                                        op0=ALU.mult, op1=ALU.mult, accum_out=sm(16)),
                   [r_xs], [r_xsb[s2], rsm(16)])
                op("act", lambda e: e.activation(out=sm(17), in_=sm(16), func=AF.Ln, scale=1.0 / D, bias=EPS),
                   [rsm(16)], [rsm(17)])
                op("act", lambda e: e.activation(out=sm(18), in_=sm(17), func=AF.Exp, scale=-0.5), [rsm(17)], [rsm(18)])
                op("dve", lambda e, xsrc=xsrc, s2=s2: e.tensor_scalar(out=xsb[s2][:], in0=xsrc, scalar1=sm(18),
                                                                 scalar2=None, op0=ALU.mult),
                   [r_xs, rsm(18)], [r_xsb[s2]])
                for kc in range(8):
                    op("pe", lambda e, kc=kc, s2=s2: e.transpose(out=ptr[:, kc * 128:(kc + 1) * 128],
                                                                  in_=xsb[s2][:, kc * 128:(kc + 1) * 128],
                                                                  identity=ident_bf[:]),
                       [r_xsb[s2], r_idb], [r_ptr], signal=(kc == 7))
                op("dve", lambda e, j=j: e.tensor_tensor(out=xTt[:, :, j * 128:(j + 1) * 128],
                                                        in0=ptr[:].rearrange("p (a b) -> p a b", a=8), in1=gpre_bc,
                                                        op=ALU.mult), [r_ptr, r_const], [r_x])

        def phase1(bi, kind):
            xTt, r_x = xT[bi % 2], r_xT[bi % 2]
            n = TB if kind == "p" else 128
            ring_ids[0] = list(getattr(Sched, "RING1", [0, 1, 2, 3, 4]))
            v3 = lambda ap: ap.rearrange("p (a b) -> p a b", b=8)
            for ct in range(8):
                k = ct % 2
                w, r_w = load_w([(ct * 128, 128), (1024 + ct * 128, 128)])
                px, r_px = ring_next()
                proj(w, r_w, 0, 128, xTt, r_x, 0, n, px[:, 0:n], r_px)
                pz, r_pz = ring_next()
                proj(w, r_w, 128, 128, xTt, r_x, 0, n, pz[:, 0:n], r_pz)
                if kind == "p":
                    xl, r_xl = XL[k], r_XL[k]
                    op("pool", lambda e: e.tensor_copy(out=xl[:, 0:3], in_=cst3[:, ct, :]), [r_cst3[ct]], [r_xl])
                    op("act", lambda e: e.activation(out=xl[:, 3:3 + n], in_=px[:, 0:n], func=AF.Copy), [r_px], [r_xl])
                    op("pool", lambda e: e.tensor_copy(out=cst3[:, ct, :], in_=xl[:, n:n + 3]), [r_xl], [r_cst3[ct]])
                    if last_p[0]:
                        op("act", lambda e: e.activation(out=cfin[:, ct, 0, :], in_=px[:, n - 3:n], func=AF.Copy),
                           [r_px], [r_fin])
                    taps = [xl[:, j:j + n] for j in range(4)]
                else:
                    xl, r_xl = XLs, r_XLs
                    op("pool", lambda e: e.tensor_copy(out=XLs[:, :, 0:3], in_=shc[:, ct, :, :]), [r_const], [r_xl])
                    op("act", lambda e: e.activation(out=XLs[:, :, 3:11], in_=v3(px[:, 0:n]), func=AF.Copy), [r_px], [r_xl])
                    op("act", lambda e: e.activation(out=cfin[:, ct, 1:17, :], in_=v3(px[:, 0:n])[:, :, 5:8], func=AF.Copy),
                       [r_px], [r_fin])
                    taps = [XLs[:, :, j:j + 8] for j in range(4)]
                xc_v = XC[k][:, 0:n] if kind == "p" else v3(XC[k][:, 0:n])
                pcv, r_pc = XC[k], r_XC[k]
                op("act", lambda e: e.activation(out=XC[k][:, 0:n], in_=px[:, 0:n], func=AF.Identity,
                                                 scale=pc(PC_CW + 3 * 8 + ct), bias=pc(PC_CB + ct)), [r_px, r_const], [r_XC[k]])
                for j in range(3):
                    wj = pc(PC_CW + j * 8 + ct)
                    op("dve", lambda e, j=j, wj=wj: e.scalar_tensor_tensor(out=xc_v, in0=taps[j], scalar=wj, in1=xc_v,
                                                                       op0=ALU.mult, op1=ALU.add),
                       [r_xl, r_XC[k], r_const], [r_XC[k]])
                op("dve", lambda e: e.tensor_copy(out=XCB[k][:, 0:n], in_=XC[k][:, 0:n]), [r_XC[k]], [r_XCB[k]])
                op("act", lambda e: e.activation(out=TZ[k][:, 0:n], in_=pz[:, 0:n], func=AF.Tanh, scale=0.5),
                   [r_pz], [r_TZ[k]])
                op("dve", lambda e: e.scalar_tensor_tensor(out=SZ[k][:, 0:n], in0=TZ[k][:, 0:n], scalar=1.0,
                                                           in1=pz[:, 0:n], op0=ALU.add, op1=ALU.mult),
                   [r_TZ[k], r_pz], [r_SZ[k]])
                pr_, r_pr = ring_next()
                op("pe", lambda e: e.matmul(pr_[:, 0:n], lhsT=wg_bf[:, ct * 128:(ct + 1) * 128], rhs=XCB[k][:, 0:n],
                                            start=True, stop=True), [r_wgbf, r_XCB[k]], [r_pr])
                pi_, r_pi = ring_next()
                op("pe", lambda e: e.matmul(pi_[:, 0:n], lhsT=wg_bf[:, (8 + ct) * 128:(9 + ct) * 128], rhs=XCB[k][:, 0:n],
                                            start=True, stop=True), [r_wgbf, r_XCB[k]], [r_pi])
                op("act", lambda e: e.activation(out=TR[k][:, 0:n], in_=pr_[:, 0:n], func=AF.Tanh, scale=0.5,
                                                 bias=dc(DC_HBRG + ct)), [r_pr, r_dcols], [r_TR[k]])
                op("act", lambda e: e.activation(out=TI[k][:, 0:n], in_=pi_[:, 0:n], func=AF.Tanh, scale=0.5,
                                                 bias=dc(DC_HBIG + ct)), [r_pi, r_dcols], [r_TI[k]])
                op("dve", lambda e: e.scalar_tensor_tensor(out=TI[k][:, 0:n], in0=TI[k][:, 0:n], scalar=1.0,
                                                           in1=pcv[:, 0:n], op0=ALU.add, op1=ALU.mult),
                   [r_TI[k], r_pc], [r_TI[k]])
                if kind == "p":
                    a_out, r_a = AA[k][:, 0:n], r_AA[k]
                    u_out, r_u = UU[k][:, 0:n], r_UU[k]
                    a_in, m_in, ip_in = TR[k][:, 0:n], A2[k][:, 0:n], TI[k][:, 0:n]
                else:
                    a_out, r_a = AAs[:, :, 1:9], r_AAs
                    u_out, r_u = UUs[:, :, 1:9], r_UUs
                    a_in, m_in, ip_in = v3(TR[k][:, 0:n]), v3(A2[k][:, 0:n]), v3(TI[k][:, 0:n])
                op("act", lambda e: e.activation(out=a_out, in_=a_in, func=AF.Exp, scale=dc(DC_CH + ct),
                                                 bias=dc(DC_CH + ct)), [r_TR[k], r_dcols], [r_a])
                op("dve", lambda e: e.tensor_tensor(out=m_in, in0=a_out, in1=a_out, op=ALU.mult), [r_a], [r_A2[k]])
                op("act", lambda e: e.activation(out=A2[k][:, 0:n], in_=A2[k][:, 0:n], func=AF.Ln, scale=-0.25,
                                                 bias=0.25), [r_A2[k]], [r_A2[k]])
                op("act", lambda e: e.activation(out=A2[k][:, 0:n], in_=A2[k][:, 0:n], func=AF.Exp, scale=0.5),
                   [r_A2[k]], [r_A2[k]])
                op("dve", lambda e: e.tensor_tensor(out=u_out, in0=m_in, in1=ip_in, op=ALU.mult),
                   [r_A2[k], r_TI[k]], [r_u])
                if kind == "p":
                    op("dve", lambda e: e.tensor_tensor_scan(out=HH[k][:, 0:n], data0=AA[k][:, 0:n], data1=UU[k][:, 0:n],
                                                             initial=hst[:, ct:ct + 1], op0=ALU.mult, op1=ALU.add),
                       [r_AA[k], r_UU[k], r_hst[ct]], [r_HH[k]])
                    op("pool", lambda e: e.tensor_copy(out=hst[:, ct:ct + 1], in_=HH[k][:, n - 1:n]),
                       [r_HH[k]], [r_hst[ct]])
                    op("dve", lambda e: e.scalar_tensor_tensor(out=yTL[bi % 2][:, ct, 0:n], in0=HH[k][:, 0:n], scalar=0.5,
                                                               in1=SZ[k][:, 0:n], op0=ALU.mult, op1=ALU.mult),
                       [r_HH[k], r_SZ[k]], [r_yTL[bi % 2][ct]])
                    if last_p[0]:
                        op("pool", lambda e: e.tensor_copy(out=hfin[:, ct, 0:1], in_=HH[k][:, n - 1:n]),
                           [r_HH[k]], [r_fin])
                else:
                    op("pool", lambda e: e.tensor_copy(out=UUs[:, :, 0], in_=shh[:, ct, :]), [r_const], [r_UUs])
                    flat = lambda t: t[:].rearrange("p a b -> p (a b)")
                    op("dve", lambda e: e.tensor_tensor_scan(out=flat(HHs), data0=flat(AAs), data1=flat(UUs),
                                                             initial=0.0, op0=ALU.mult, op1=ALU.add),
                       [r_AAs, r_UUs], [r_HHs])
                    op("dve", lambda e: e.scalar_tensor_tensor(
                        out=v3(yTL[bi % 2][:, ct, 0:n]), in0=HHs[:, :, 1:9], scalar=0.5,
                        in1=v3(SZ[k][:, 0:n]), op0=ALU.mult, op1=ALU.mult),
                       [r_HHs, r_SZ[k]], [r_yTL[bi % 2][ct]])
                    op("pool", lambda e: e.tensor_copy(out=hfin[:, ct, 1:17], in_=HHs[:, :, 8]), [r_HHs], [r_fin])

        s0_i = [0]

        def phase2(bi, kind):
            xTt, r_x = xT[bi % 2], r_xT[bi % 2]
            n = TB if kind == "p" else 128
            nch = n // 128
            PA2 = getattr(Sched, "PA2", True) and kind == "p"
            ring_ids[0] = [0, 1] if PA2 else [0, 1, 2]
            w, r_w = load_w([(5120, 16)])
            prg, r_prg = ring_next()
            proj(w, r_w, 0, 16, xTt, r_x, 0, n, prg[0:16, 0:n], r_prg)
            op("act", lambda e: e.activation(out=RGK[:, 0:n], in_=prg[0:16, 0:n], func=AF.Copy), [r_prg], [r_RGK])
            for h in range(4):
                wC, r_wC = load_w([(4096 + h * 256, 256)])
                for vh in range(2):
                    pzg, r_pzg = ring_next()
                    proj(wC, r_wC, vh * 128, 128, xTt, r_x, 0, n, pzg[:, 0:n], r_pzg)
                    tz, r_tz = TZ[vh], r_TZ[vh]
                    kk = 8 + 2 * h + vh
                    op("act", lambda e, pzg=pzg, tz=tz: e.activation(out=tz[:, 0:n], in_=pzg[:, 0:n], func=AF.Tanh, scale=0.5),
                       [r_pzg], [r_tz])
                    op("dve", lambda e, pzg=pzg, tz=tz, kk=kk: e.scalar_tensor_tensor(
                        out=yTG[:, kk - 8, 0:n], in0=tz[:, 0:n], scalar=1.0, in1=pzg[:, 0:n], op0=ALU.add, op1=ALU.mult),
                       [r_tz, r_pzg], [r_yTG[kk - 8]])
            for h in range(4):
                wA, r_wA = load_w([(2048 + h * 128, 128), (2560 + h * 128, 128)])
                wB, r_wB = load_w([(3072 + h * 256, 256)])
                if bi == 0:
                    load_wout(4)
                pg, r_pg = ring_next()
                op("pe", lambda e: e.matmul(pg[:, 0:n], lhsT=wgk2_bf[0:16, h * 128:(h + 1) * 128], rhs=RGK[0:16, 0:n],
                                            start=True, stop=True), [r_wgk2, r_RGK], [r_pg])
                EE, r_EE = EEb, r_EEb
                BB, r_BB = BBb, r_BBb
                E1, r_E1 = E123[0], r_E123[0]
                E2, r_E2 = E123[1], r_E123[1]
                E3, r_E3 = E123[2], r_E123[2]
                DD, r_DD = EEb, r_EEb
                op("act", lambda e: e.activation(out=EE[:, 0:n], in_=pg[:, 0:n], func=AF.Exp, scale=-1.0,
                                                 bias=dc(DC_NBGK + h)), [r_pg, r_dcols], [r_EE])
                op("act", lambda e: e.activation(out=EE[:, 0:n], in_=EE[:, 0:n], func=AF.Ln, bias=1.0), [r_EE], [r_EE])
                mcol = CS_CM if kind == "p" else CS_SM
                op("dve", lambda e: e.tensor_tensor_scan(out=BB[:, 0:n], data0=cst[:, mcol:mcol + n], data1=EE[:, 0:n],
                                                         initial=0.0, op0=ALU.mult, op1=ALU.add),
                   [r_EE, r_const], [r_BB])
                cl = 128 if kind == "p" else 8
                ncl = n // cl
                b3 = BB[:, 0:n].rearrange("p (a b) -> p a b", b=cl)
                op("dve", lambda e: e.tensor_tensor(out=DD[:, 0:n].rearrange("p (a b) -> p a b", b=cl), in0=b3,
                                                    in1=b3[:, :, cl - 1:cl].to_broadcast([128, ncl, cl]),
                                                    op=ALU.subtract), [r_BB], [r_DD])
                op("act", lambda e: e.activation(out=E1[:, 0:n], in_=BB[:, 0:n], func=AF.Exp, scale=-1.0 / 16),
                   [r_BB], [r_E1])
                op("act", lambda e: e.activation(out=E2[:, 0:n], in_=BB[:, 0:n], func=AF.Exp, scale=1.0 / 16),
                   [r_BB], [r_E2])
                op("act", lambda e: e.activation(out=E3[:, 0:n], in_=DD[:, 0:n], func=AF.Exp, scale=1.0 / 16),
                   [r_DD], [r_E3])
                op("act", lambda e: e.activation(out=DEC[:, 0:ncl], in_=b3[:, :, cl - 1], func=AF.Exp, scale=-1.0 / 16),
                   [r_BB], [r_DEC])
                pq, r_pq = ring_next()
                proj(wA, r_wA, 0, 128, xTt, r_x, 0, n, pq[:, 0:n], r_pq)
                op("dve", lambda e: e.scalar_tensor_tensor(out=QT[:, 0:n], in0=pq[:, 0:n], scalar=QSCALE, in1=E1[:, 0:n],
                                                           op0=ALU.mult, op1=ALU.mult), [r_pq, r_E1], [r_QT])
                pk, r_pk = ring_next()
                proj(wA, r_wA, 128, 128, xTt, r_x, 0, n, pk[:, 0:n], r_pk)
                op("dve", lambda e: e.tensor_tensor(out=KT[:, 0:n], in0=pk[:, 0:n], in1=E2[:, 0:n], op=ALU.mult),
                   [r_pk, r_E2], [r_KT])
                op("dve", lambda e: e.tensor_tensor(out=KE[:, 0:n], in0=pk[:, 0:n], in1=E3[:, 0:n], op=ALU.mult),
                   [r_pk, r_E3], [r_KE])
                for j in range(nch):
                    pv = pb_v[:, (j % 2) * 256:(j % 2 + 1) * 256]
                    for kc in range(8):
                        op("pe", lambda e, kc=kc, j=j, pv=pv: e.matmul(pv, lhsT=xTt[:, kc, j * 128:(j + 1) * 128],
                                                                      rhs=wB[:, kc, 0:256], start=(kc == 0), stop=(kc == 7)),
                           [r_wB, r_x], [r_pbv[j % 2]], signal=(kc == 7))
                    op("dve", lambda e, j=j, pv=pv: e.tensor_copy(out=VS[:, j, :], in_=pv),
                       [r_pbv[j % 2]], [r_VS[j]])
                for c in range(nch):
                    k = c % 2
                    pai = (4 if c % 2 == 0 else 2) if PA2 else 4
                    pba_c, r_pba_c = bank[pai], r_bank[pai]
                    cs = slice(c * 128, (c + 1) * 128)
                    op("pe", lambda e: e.transpose(out=ptr[:, 0:128], in_=KE[:, cs], identity=ident_bf[:]),
                       [r_KE, r_idb], [r_ptr])
                    op("dve", lambda e: e.tensor_copy(out=KEt[k][:], in_=ptr[:, 0:128]), [r_ptr], [r_KEt[k]])
                    op("pe", lambda e: e.matmul(pba_c[:, 0:128], lhsT=KT[:, cs], rhs=QT[:, cs], start=True, stop=True),
                       [r_KT, r_QT], [r_pba_c])
                    mk = CS_U if kind == "p" else CS_US
                    op("dve", lambda e: e.tensor_tensor(out=ATT[k][:], in0=pba_c[:, 0:128], in1=cst[:, mk:mk + 128],
                                                        op=ALU.mult), [r_pba_c, r_const], [r_ATT[k]])
                    if kind == "p":
                        pbi = 5 + (c % 2 if getattr(Sched, "PO2", True) else 0)
                        pbo = bank[pbi]
                        pos = (pbo[:, 0:128], pbo[:, 128:256])
                        r_pos = [r_bank[pbi], r_bank[pbi]]
                    else:
                        pos = (pb_o0[:, 0:128], pb_o1[:, 0:128])
                        r_pos = [r_bank[5], r_bank[6]]
                    if kind == "p":
                        for vh in range(2):
                            op("pe", lambda e, vh=vh: e.matmul(pos[vh], lhsT=VS[:, c, vh * 128:(vh + 1) * 128], rhs=ATT[k][:],
                                                               start=True, stop=False), [r_VS[c], r_ATT[k]], [r_pos[vh]],
                               signal=False)
                            op("pe", lambda e, vh=vh: e.matmul(pos[vh], lhsT=Sbf[:, h, vh * 128:(vh + 1) * 128], rhs=QT[:, cs],
                                                               start=False, stop=True), [r_Sbf[h], r_QT], [r_pos[vh]])
                        pu = pb_v[:, 0:256]
                        op("pe", lambda e: e.matmul(pu, lhsT=KEt[k][:], rhs=VS[:, c, :], start=True, stop=True),
                           [r_KEt[k], r_VS[c]], [r_pbv[0]])
                        op("dve", lambda e: e.scalar_tensor_tensor(out=Sst[:, h, :], in0=Sst[:, h, :], scalar=DEC[:, c:c + 1],
                                                                   in1=pu, op0=ALU.mult, op1=ALU.add),
                           [r_S[h], r_DEC, r_pbv[0]], [r_S[h]])
                        op("act", lambda e: e.activation(out=Sbf[:, h, :], in_=Sst[:, h, :], func=AF.Copy), [r_S[h]], [r_Sbf[h]])
                        if last_p[0] and c == nch - 1:
                            dma("pool", os_d[0, h], Sst[:, h, :], [r_S[h]], [], "os_p", is_output=True)
                    else:
                        qm_out = bass.AP(QM.tensor, QM.offset, [[QM.ap[0][0], 128], [136, NSEQ], [1, 8]])
                        op("dve", lambda e: e.tensor_copy(out=qm_out, in_=QT[:, 0:128].rearrange("p (a b) -> p a b", b=8)),
                           [r_QT], [r_QM])
                        kem3 = KEM.rearrange("p (a b) -> p a b", b=128)
                        op("dve", lambda e: e.tensor_tensor(
                            out=kem3, in0=KEt[k][:].unsqueeze(1).to_broadcast([128, NSEQ, 128]),
                            in1=cst[:, CS_M:CS_M + NSEQ].unsqueeze(2).to_broadcast([128, NSEQ, 128]), op=ALU.mult),
                           [r_KEt[k], r_const], [r_KEM])
                        for vh in range(2):
                            op("pe", lambda e, vh=vh: e.matmul(pos[vh], lhsT=VS[:, c, vh * 128:(vh + 1) * 128], rhs=ATT[k][:],
                                                               start=True, stop=False), [r_VS[c], r_ATT[k]], [r_pos[vh]],
                               signal=False)
                        for i in range(NSEQ):
                            s = s0_i[0] % NS0
                            s0_i[0] += 1
                            dma("sp", S0[s], sgla_d[i, h], [], [r_S0[s]], f"S0{s}")
                            sbi = s % NSB
                            op("act", lambda e, s=s, sbi=sbi: e.activation(out=S0b[sbi][:], in_=S0[s], func=AF.Copy), [r_S0[s]], [r_S0b[sbi]])
                            for vh in range(2):
                                op("pe", lambda e, vh=vh, s=s, i=i, sbi=sbi: e.matmul(
                                    pos[vh], lhsT=S0b[sbi][:, vh * 128:(vh + 1) * 128], rhs=QM[:, i * 128:(i + 1) * 128],
                                    start=False, stop=(i == NSEQ - 1)), [r_S0b[sbi], r_QM], [r_pos[vh]],
                                   signal=(i == NSEQ - 1))
                            pub = [0, 1, 2, 3][i % 4]
                            pu = bank[pub][:, 0:256]
                            op("pe", lambda e, i=i, pu=pu: e.matmul(pu, lhsT=KEM[:, i * 128:(i + 1) * 128], rhs=VS[:, c, :],
                                                                    start=True, stop=True), [r_KEM, r_VS[c]], [r_bank[pub]])
                            op("dve", lambda e, s=s, i=i, pu=pu: e.scalar_tensor_tensor(
                                out=S0[s], in0=S0[s], scalar=DEC[:, i:i + 1], in1=pu, op0=ALU.mult, op1=ALU.add),
                               [r_S0[s], r_DEC, r_bank[pub]], [r_S0[s]])
                            dma("sp", os_d[1 + i, h], S0[s], [r_S0[s]], [], f"SNo{s}", is_output=True)
                    if kind == "p":
                        op("act", lambda e: e.activation(out=SQ[k][:], in_=pbo[:, 0:256], func=AF.Square),
                           [r_pos[0]], [r_SQ[k]])
                    else:
                        for vh in range(2):
                            op("act", lambda e, vh=vh: e.activation(out=SQ[k][:, vh * 128:(vh + 1) * 128], in_=pos[vh],
                                                                    func=AF.Square), [r_pos[vh]], [r_SQ[k]])
                    op("pe", lambda e: e.matmul(pba_c[:, 128:256], lhsT=ones_bf[:], rhs=SQ[k][:, 0:128], start=True, stop=False),
                       [r_idb, r_SQ[k]], [r_pba_c], signal=False)
                    op("pe", lambda e: e.matmul(pba_c[:, 128:256], lhsT=ones_bf[:], rhs=SQ[k][:, 128:256], start=False, stop=True),
                       [r_idb, r_SQ[k]], [r_pba_c])
                    op("act", lambda e: e.activation(out=RT[k][:], in_=pba_c[:, 128:256], func=AF.Ln, scale=1.0 / HV, bias=EPS),
                       [r_pba_c], [r_RT[k]])
                    op("act", lambda e: e.activation(out=RS[k][:], in_=RT[k][:], func=AF.Exp, scale=-0.5), [r_RT[k]], [r_RS[k]])
                    for vh in range(2):
                        kk = 8 + 2 * h + vh
                        op("dve", lambda e, vh=vh: e.scalar_tensor_tensor(out=Y1[vh][:], in0=pos[vh], scalar=dc(DC_GH2 + vh),
                                                                         in1=RS[k][:], op0=ALU.mult, op1=ALU.mult),
                           [r_pos[vh], r_RS[k], r_dcols], [r_Y1[vh]])
                        op("dve", lambda e, vh=vh, kk=kk: e.tensor_tensor(out=yTG[:, kk - 8, cs], in0=Y1[vh][:], in1=yTG[:, kk - 8, cs],
                                                                        op=ALU.mult), [r_Y1[vh], r_yTG[kk - 8]], [r_yTG[kk - 8]])

        xr_i = [0]

        def phase3(bi, row0, ntile):
            ring_ids[0] = list(getattr(Sched, "RING3", [7, 5, 6]))
            for j in range(ntile):
                xs_ = xi_i[0] % NXI
                xi_i[0] += 1
                s = 0
                rows = slice(row0 + j * 128, row0 + (j + 1) * 128)
                dma("sp", xin[xs_][:], x_d[rows, :], [], [r_xin[xs_]], f"xin{xs_}")
                pouts = []
                for half in range(2):
                    po_, r_po_ = ring_next()
                    for kc in range(16):
                        ysrc = yTL[bi % 2][:, kc, j * 128:(j + 1) * 128] if kc < 8 else yTG[:, kc - 8, j * 128:(j + 1) * 128]
                        r_ys = r_yTL[bi % 2][kc] if kc < 8 else r_yTG[kc - 8]
                        op("pe", lambda e, kc=kc, half=half, po_=po_, ysrc=ysrc: e.matmul(
                            po_[:], lhsT=ysrc, rhs=wout_bf[:, kc, half * 512:(half + 1) * 512],
                            start=(kc == 0), stop=(kc == 15)), [r_ys, r_wout_k[kc]], [r_po_], signal=(kc == 15))
                    op("act", lambda e, half=half, po_=po_: e.activation(out=junk[:], in_=po_[:],
                                                                        func=AF.Square, accum_out=sm(20 + half)),
                       [r_po_], [r_junk, rsm(20 + half)])
                    pouts.append((po_, r_po_))
                op("dve", lambda e: e.tensor_tensor(out=sm(22), in0=sm(20), in1=sm(21), op=ALU.add),
                   [rsm(20), rsm(21)], [rsm(22)])
                op("act", lambda e: e.activation(out=sm(23), in_=sm(22), func=AF.Ln, scale=1.0 / D, bias=EPS),
                   [rsm(22)], [rsm(23)])
                op("act", lambda e: e.activation(out=sm(24), in_=sm(23), func=AF.Exp, scale=-0.5), [rsm(23)], [rsm(24)])
                for half in range(2):
                    po_, r_po_ = pouts[half]
                    hs = slice(half * 512, (half + 1) * 512)
                    op("dve", lambda e, po_=po_, hs=hs, s=s: e.scalar_tensor_tensor(
                        out=osb[s][:, hs], in0=po_[:], scalar=sm(24), in1=gpost[:, hs], op0=ALU.mult, op1=ALU.mult),
                       [r_po_, rsm(24), r_const], [r_osb[s]])
                op("dve", lambda e, s=s, xs_=xs_: e.tensor_tensor(out=osb[s][:], in0=osb[s][:], in1=xin[xs_][:], op=ALU.add),
                   [r_osb[s], r_xin[xs_]], [r_osb[s]])
                dma("sp", y_d[rows, :], osb[s][:], [r_osb[s]], [], f"osb{s}", is_output=True)

        pblocks = [("p", b * TB, TB // 128) for b in range(NPB)]
        spos = getattr(Sched, "SPOS", 4)
        assert spos <= 4
        blocks = pblocks[:spos] + [("s", LP, 1)] + pblocks[spos:]
        phase0(0, blocks[0][1], blocks[0][2])
        late_setup()
        for bi, (kind, row0, ntile) in enumerate(blocks):
            first_pass[0] = (bi == 0)
            last_p[0] = (kind == "p" and row0 == LP - TB)
            S.tag = f"b{bi}"
            if kind == "s":
                op("pool", lambda e: e.memset(S0b[0][:, 0:1], 0.0), [], r_ax + [r_QM, r_KEM, r_S0b[0]] + r_S0)
                op("pool", lambda e: e.memset(QM, 0.0), [], [r_QM])
            phase1(bi, kind)
            phase2(bi, kind)
            if bi + 1 < len(blocks):
                phase0(bi + 1, blocks[bi + 1][1], blocks[bi + 1][2])
            phase3(bi, row0, ntile)
        dma("pool", oh_d, hfin[:].rearrange("p a b -> p (a b)"), [r_fin], [], "fin", is_output=True)
        dma("pool", oc_d, cfin[:].rearrange("p a b c -> p (a b c)"), [r_fin], [], "fin", is_output=True)
        S.finish()
        tg = {}
        for nd in S.nodes:
            t = tg.setdefault((nd["tag"], nd["eng"]), [1e9, 0.0, 0.0])
            t[0] = min(t[0], nd["start"]); t[1] = max(t[1], nd["fin"]); t[2] += nd["cost"]
        build_nc.stats = dict(nodes=len(S.nodes), nwait=S.nwait, est_us=S.est_total, tags=tg)
    return nc


_NC_CACHE = {}


def _consts():
    c = np.zeros((128, CS_N), np.float32)
    c[:, CS_ID:CS_ID + 128] = np.eye(128, dtype=np.float32)
    s = np.arange(128)[:, None]
    t = np.arange(128)[None, :]
    c[:, CS_U:CS_U + 128] = (s <= t)
    c[:, CS_US:CS_US + 128] = (s <= t) & (s // SL == t // SL)
    c[:, CS_M:CS_M + NSEQ] = (s // SL == np.arange(NSEQ)[None, :])
    cm = np.ones(TB, np.float32)
    cm[::128] = 0.0
    c[:, CS_CM:CS_CM + TB] = cm[None, :]
    smk = np.ones(128, np.float32)
    smk[::SL] = 0.0
    c[:, CS_SM:CS_SM + 128] = smk[None, :]
    return c


def make_in_maps(x_prompt, x_sample, state_lru_h, state_lru_conv, state_gla, g_pre, w_in, conv_w, conv_b,
           w_rg, b_rg, w_ig, b_ig, lru_lambda, w_gk2, b_gk, g_head, w_out, g_post):
    f = lambda a: np.ascontiguousarray(np.asarray(a, dtype=np.float32))
    x_prompt, x_sample = f(x_prompt), f(x_sample)
    col = lambda v, n: f(v).reshape(n, 128).T
    pcols = np.concatenate([
        col(g_pre[0], 8),
        np.concatenate([col(conv_w[0][j], 8) for j in range(4)], axis=1),
        col(conv_b[0], 8), col(b_rg[0], 8), col(b_ig[0], 8), col(lru_lambda[0], 8),
        col(b_gk[0], 4), col(g_head[0], 2)], axis=1)
    pcols = np.ascontiguousarray(pcols, dtype=np.float32)
    assert pcols.shape == (128, PC_N)
    wg = np.zeros((128, 2, 8, 128), np.float32)
    for gi, wsrc in enumerate((f(w_rg)[0], f(w_ig)[0])):
        for ct in range(8):
            wg[0:64, gi, ct, 0:64] = wsrc[2 * ct]
            wg[64:128, gi, ct, 64:128] = wsrc[2 * ct + 1]
    wg = wg.reshape(128, 2 * 8 * 128)
    gpost = np.ascontiguousarray(np.broadcast_to(f(g_post)[0][None, :], (128, D)))
    cst = _consts()
    win = f(w_in)[0]
    wout = f(w_out)[0]
    wgk2 = f(w_gk2)[0]
    in_maps = []
    for c in range(NCORES):
        sl = slice(c * NSEQ, (c + 1) * NSEQ)
        xs = np.concatenate([x_prompt[c], x_sample[sl].reshape(NSEQ * SL, D)], axis=0)
        shc = f(state_lru_conv)[0, sl]
        shc = shc.reshape(NSEQ, 3, 8, 128).transpose(3, 2, 0, 1).reshape(128, 8 * NSEQ * 3)
        shh = f(state_lru_h)[0, sl].reshape(NSEQ, 8, 128).transpose(2, 1, 0).reshape(128, 8 * NSEQ)
        in_maps.append({
            "x": np.ascontiguousarray(xs), "w_in": win, "w_out": wout, "wg": wg, "wgk2": wgk2, "pcols": pcols,
            "cst": cst, "gpost": gpost, "shc": np.ascontiguousarray(shc), "shh": np.ascontiguousarray(shh),
            "sgla": np.ascontiguousarray(f(state_gla)[0, sl]),
        })
    return in_maps


def kernel(**inputs):
    in_maps = make_in_maps(**inputs)
    if "nc" not in _NC_CACHE:
        _NC_CACHE["nc"] = build_nc()
    res = run_bass_kernel_spmd(_NC_CACHE["nc"], in_maps, core_ids=list(range(NCORES)))
    return assemble(res.results)


def assemble(R):
    y_p = np.stack([R[c]["y"][:LP] for c in range(NCORES)], 0)
    y_s = np.concatenate([R[c]["y"][LP:].reshape(NSEQ, SL, D) for c in range(NCORES)], 0)
    oh = [R[c]["o_h"].reshape(128, 8, 17).transpose(2, 1, 0).reshape(17, D) for c in range(NCORES)]
    oc = [R[c]["o_c"].reshape(128, 8, 17, 3).transpose(2, 3, 1, 0).reshape(17, 3, D) for c in range(NCORES)]
    h_p = np.stack([o[0] for o in oh], 0)[None]
    h_s = np.concatenate([o[1:] for o in oh], 0)[None]
    c_p = np.stack([o[0] for o in oc], 0)[None]
    c_s = np.concatenate([o[1:] for o in oc], 0)[None]
    s_p = np.stack([R[c]["o_s"][0] for c in range(NCORES)], 0)[None]
    s_s = np.concatenate([R[c]["o_s"][1:] for c in range(NCORES)], 0)[None]
    out = (y_p, y_s, h_p, c_p, s_p, h_s, c_s, s_s)
    return tuple(np.ascontiguousarray(o, dtype=np.float32) for o in out)
```
